# Optimizing a Trainium2 kernel written in Bass

```python
import math
import jax, jax.numpy as jnp
from jax import lax
import numpy as np

D_MODEL = 1024
BATCH = 8
SEQ = 2048
DEPTH = 4

HEAD_DIM = 64
DIL_GROUPS = ((128, 1), (512, 4), (2048, 16))
N_DIL = len(DIL_GROUPS)
HEADS_PER_DIL = 4
HA = N_DIL * HEADS_PER_DIL
HB = 6
HC = 6
N_HEADS = HA + HB + HC
MIX_WIDTH = N_HEADS * HEAD_DIM
BAND_BLOCK = 128
MOBA_BLOCK = 256
MOBA_TOPK = 3
MOBA_CHUNK = 32
SB_BLOCK = 128
N_BRANCH = 3
D_FF = -(-8 * D_MODEL // (3 * 256)) * 256
ROPE_THETA = 10000.0
NORM_EPS = 1e-6

kernel_name = "hybrid_gated_dilated_moba_stickbreaking_block"


def rms_norm(x, g):
    xf = x.astype(jnp.float32)
    y = xf * lax.rsqrt(jnp.mean(xf * xf, axis=-1, keepdims=True) + NORM_EPS)
    return (y * g.astype(jnp.float32)).astype(x.dtype)


def rope_tables(seq):
    pos = jnp.arange(seq, dtype=jnp.float32)
    inv = ROPE_THETA ** (-jnp.arange(0, HEAD_DIM, 2, dtype=jnp.float32) / HEAD_DIM)
    ang = pos[:, None] * inv[None, :]
    return jnp.cos(ang), jnp.sin(ang)


def apply_rope(x, cos, sin):
    x1, x2 = jnp.split(x, 2, axis=-1)
    cos = cos.astype(x.dtype)
    sin = sin.astype(x.dtype)
    return jnp.concatenate([x1 * cos - x2 * sin, x1 * sin + x2 * cos], axis=-1)


def _softmax_lse(s):
    mx = jnp.max(s, axis=-1, keepdims=True)
    e = jnp.exp(s - mx)
    den = jnp.sum(e, axis=-1, keepdims=True)
    return e / den, (mx + jnp.log(den))[..., 0]


def dilated_window_attention(q, k, v, window, dilation):
    B, H, S, Dh = q.shape
    L = S // dilation
    span = window // dilation
    blk = BAND_BLOCK
    nb = -(-L // blk)
    pad_r = nb * blk - L

    def to_sub(a):
        return a.reshape(B, H, L, dilation, Dh).transpose(0, 1, 3, 2, 4)

    qs, ks, vs = to_sub(q), to_sub(k), to_sub(v)
    qs = jnp.pad(qs, ((0, 0), (0, 0), (0, 0), (0, pad_r), (0, 0)))
    ks = jnp.pad(ks, ((0, 0), (0, 0), (0, 0), (blk, pad_r), (0, 0)))
    vs = jnp.pad(vs, ((0, 0), (0, 0), (0, 0), (blk, pad_r), (0, 0)))
    qb = qs.reshape(B, H, dilation, nb, blk, Dh)
    kb = ks.reshape(B, H, dilation, nb + 1, blk, Dh)
    vb = vs.reshape(B, H, dilation, nb + 1, blk, Dh)
    kw = jnp.concatenate([kb[:, :, :, :-1], kb[:, :, :, 1:]], axis=4)
    vw = jnp.concatenate([vb[:, :, :, :-1], vb[:, :, :, 1:]], axis=4)
    m = jnp.arange(nb)[:, None] * blk + jnp.arange(blk)[None, :]
    n = jnp.arange(nb)[:, None] * blk - blk + jnp.arange(2 * blk)[None, :]
    rel = m[:, :, None] - n[:, None, :]
    mask = (rel >= 0) & (rel <= span) & (n[:, None, :] >= 0)
    s = jnp.einsum("bhrnqd,bhrnkd->bhrnqk", qb, kw).astype(jnp.float32) * (HEAD_DIM ** -0.5)
    s = jnp.where(mask, s, -jnp.inf)
    p, lse = _softmax_lse(s)
    o = jnp.einsum("bhrnqk,bhrnkd->bhrnqd", p.astype(v.dtype), vw)
    o = o.reshape(B, H, dilation, nb * blk, Dh)[:, :, :, :L]
    o = o.transpose(0, 1, 3, 2, 4).reshape(B, H, S, Dh)
    lse = lse.reshape(B, H, dilation, nb * blk)[:, :, :, :L]
    lse = lse.transpose(0, 1, 3, 2).reshape(B, H, S)
    return o, lse


def dilated_mixture(qa, ka, va):
    outs, lses = [], []
    for g, (window, dilation) in enumerate(DIL_GROUPS):
        sl = slice(g * HEADS_PER_DIL, (g + 1) * HEADS_PER_DIL)
        o, lse = dilated_window_attention(qa[:, sl], ka[:, sl], va[:, sl], window, dilation)
        outs.append(o)
        lses.append(lse)
    w = jax.nn.softmax(jnp.stack(lses, axis=0), axis=0)
    o = jnp.sum(w[..., None] * jnp.stack(outs, axis=0).astype(jnp.float32), axis=0)
    return o.astype(qa.dtype)


def moba_attention(q, k, v):
    B, H, S, Dh = q.shape
    nblk = -(-S // MOBA_BLOCK)
    pad = nblk * MOBA_BLOCK - S
    kb = jnp.pad(k, ((0, 0), (0, 0), (0, pad), (0, 0))).reshape(B, H, nblk, MOBA_BLOCK, Dh)
    vb = jnp.pad(v, ((0, 0), (0, 0), (0, pad), (0, 0))).reshape(B, H, nblk, MOBA_BLOCK, Dh)
    topk = max(1, min(MOBA_TOPK, nblk - 1))
    own_blk = jnp.arange(S) // MOBA_BLOCK
    k_mean = jnp.mean(kb.astype(jnp.float32), axis=3)
    gate = jnp.einsum("bhsd,bhnd->bhsn", q.astype(jnp.float32), k_mean)
    past = jnp.arange(nblk)[None, :] < own_blk[:, None]
    gate = jnp.where(past, gate, -jnp.inf)
    _, sel = lax.top_k(gate, topk)
    valid = sel < own_blk[:, None]
    nq = S // MOBA_CHUNK
    scale = HEAD_DIM ** -0.5
    bi = jnp.arange(B)[:, None, None, None]
    hi = jnp.arange(H)[None, :, None, None]

    def to_chunks(a):
        return jnp.moveaxis(a.reshape(B, H, nq, MOBA_CHUNK, *a.shape[3:]), 2, 0)

    def chunk(args):
        i, qi, si, vi = args
        t = i * MOBA_CHUNK + jnp.arange(MOBA_CHUNK)
        ob = (i * MOBA_CHUNK) // MOBA_BLOCK
        k_own = lax.dynamic_index_in_dim(kb, ob, axis=2, keepdims=False)
        v_own = lax.dynamic_index_in_dim(vb, ob, axis=2, keepdims=False)
        k_sel = kb[bi, hi, si]
        v_sel = vb[bi, hi, si]
        s_sel = jnp.einsum("bhqd,bhqkld->bhqkl", qi, k_sel).astype(jnp.float32) * scale
        s_sel = jnp.where(vi[..., None], s_sel, -jnp.inf).reshape(B, H, MOBA_CHUNK, topk * MOBA_BLOCK)
        s_own = jnp.einsum("bhqd,bhld->bhql", qi, k_own).astype(jnp.float32) * scale
        key_pos = ob * MOBA_BLOCK + jnp.arange(MOBA_BLOCK)
        s_own = jnp.where(key_pos[None, :] <= t[:, None], s_own, -jnp.inf)
        p, _ = _softmax_lse(jnp.concatenate([s_sel, s_own], axis=-1))
        p = p.astype(v.dtype)
        p_sel = p[..., : topk * MOBA_BLOCK].reshape(B, H, MOBA_CHUNK, topk, MOBA_BLOCK)
        p_own = p[..., topk * MOBA_BLOCK:]
        return (jnp.einsum("bhqkl,bhqkld->bhqd", p_sel, v_sel)
                + jnp.einsum("bhql,bhld->bhqd", p_own, v_own))

    o = lax.map(chunk, (jnp.arange(nq), to_chunks(q), to_chunks(sel), to_chunks(valid)))
    return jnp.moveaxis(o, 0, 2).reshape(B, H, S, Dh)


def stick_breaking_attention(q, k, v):
    B, H, S, Dh = q.shape
    nq = S // SB_BLOCK
    key_pos = jnp.arange(S)
    scale = HEAD_DIM ** -0.5
    qc = jnp.moveaxis(q.reshape(B, H, nq, SB_BLOCK, Dh), 2, 0)

    def block(args):
        i, qi = args
        t = i * SB_BLOCK + jnp.arange(SB_BLOCK)
        past = key_pos[None, :] < t[:, None]
        z = jnp.einsum("bhqd,bhsd->bhqs", qi, k).astype(jnp.float32) * scale
        log_keep = jnp.where(past, jax.nn.log_sigmoid(-z), 0.0)
        after = lax.cumsum(log_keep, axis=3, reverse=True) - log_keep
        a = jnp.where(past, jnp.exp(jax.nn.log_sigmoid(z) + after), 0.0)
        return jnp.einsum("bhqs,bhsd->bhqd", a.astype(v.dtype), v)

    o = lax.map(block, (jnp.arange(nq), qc))
    return jnp.moveaxis(o, 0, 2).reshape(B, H, S, Dh)


def hybrid_mixer(h, w_in, w_br_a, w_br_b, w_br_c, w_gate, b_gate, w_out, cos, sin):
    B, S, D = h.shape
    qkv = (h @ w_in).reshape(B, S, 3, N_HEADS, HEAD_DIM).transpose(2, 0, 3, 1, 4)
    q, k, v = qkv[0], qkv[1], qkv[2]
    n_rot = HA + HB
    q_rot = apply_rope(q[:, :n_rot], cos, sin)
    k_rot = apply_rope(k[:, :n_rot], cos, sin)
    oa = dilated_mixture(q_rot[:, :HA], k_rot[:, :HA], v[:, :HA])
    ob = moba_attention(q_rot[:, HA:], k_rot[:, HA:], v[:, HA:n_rot])
    oc = stick_breaking_attention(q[:, n_rot:], k[:, n_rot:], v[:, n_rot:])

    def flat(o):
        return o.transpose(0, 2, 1, 3).reshape(B, S, -1)

    ya = flat(oa) @ w_br_a
    yb = flat(ob) @ w_br_b
    yc = flat(oc) @ w_br_c
    gates = jax.nn.sigmoid(h @ w_gate + b_gate).reshape(B, S, N_BRANCH, D)
    merged = gates[:, :, 0] * ya + gates[:, :, 1] * yb + gates[:, :, 2] * yc
    return merged @ w_out


def swiglu(h, w_gu, w_down):
    gt, up = jnp.split(h @ w_gu, 2, axis=-1)
    return (jax.nn.silu(gt) * up) @ w_down


def setup_inputs(seed: int = 0) -> dict:
    key = jax.random.key(seed)
    ks = jax.random.split(key, 16)

    def nrm(k, shape, fan_in, gain=1.0):
        return jax.random.normal(k, shape, jnp.float32) * (gain * fan_in ** -0.5)

    def gain_vec(k, shape):
        return 1.0 + 0.05 * jax.random.normal(k, shape, jnp.float32)

    D = D_MODEL
    return {
        "x": jax.random.normal(ks[0], (BATCH, SEQ, D), jnp.float32),
        "c": jax.random.normal(ks[1], (BATCH, D), jnp.float32),
        "w_ada": nrm(ks[2], (DEPTH, D, 6 * D), D, 0.5),
        "b_ada": 0.02 * jax.random.normal(ks[3], (DEPTH, 6 * D), jnp.float32),
        "norm1_g": gain_vec(ks[4], (DEPTH, D)),
        "w_in": nrm(ks[5], (DEPTH, D, 3 * MIX_WIDTH), D),
        "w_br_a": nrm(ks[6], (DEPTH, HEADS_PER_DIL * HEAD_DIM, D), HEADS_PER_DIL * HEAD_DIM),
        "w_br_b": nrm(ks[7], (DEPTH, HB * HEAD_DIM, D), HB * HEAD_DIM),
        "w_br_c": nrm(ks[8], (DEPTH, HC * HEAD_DIM, D), HC * HEAD_DIM),
        "w_gate": nrm(ks[9], (DEPTH, D, N_BRANCH * D), D),
        "b_gate": 0.02 * jax.random.normal(ks[10], (DEPTH, N_BRANCH * D), jnp.float32),
        "w_out": nrm(ks[11], (DEPTH, D, D), D),
        "norm2_g": gain_vec(ks[12], (DEPTH, D)),
        "w_gu": nrm(ks[13], (DEPTH, D, 2 * D_FF), D),
        "w_down": nrm(ks[14], (DEPTH, D_FF, D), D_FF),
        "final_g": gain_vec(ks[15], (D,)),
    }


def reference(x, c, w_ada, b_ada, norm1_g, w_in, w_br_a, w_br_b, w_br_c, w_gate, b_gate,
              w_out, norm2_g, w_gu, w_down, final_g):
    S = x.shape[1]
    cos, sin = rope_tables(S)
    c_act = jax.nn.silu(c)
    for l in range(DEPTH):
        mod = c_act @ w_ada[l] + b_ada[l]
        sh1, sc1, g1, sh2, sc2, g2 = [m[:, None, :] for m in jnp.split(mod, 6, axis=-1)]
        h = rms_norm(x, norm1_g[l]) * (1.0 + sc1) + sh1
        x = x + g1 * hybrid_mixer(h, w_in[l], w_br_a[l], w_br_b[l], w_br_c[l],
                                  w_gate[l], b_gate[l], w_out[l], cos, sin)
        h = rms_norm(x, norm2_g[l]) * (1.0 + sc2) + sh2
        x = x + g2 * swiglu(h, w_gu[l], w_down[l])
    return rms_norm(x, final_g)
```

```python
import numpy as np
import concourse.bass as bass
import concourse.mybir as mybir
from concourse.bass_utils import run_bass_kernel_spmd

F32 = mybir.dt.float32
BF16 = mybir.dt.bfloat16
AF = mybir.ActivationFunctionType
ALU = mybir.AluOpType
AX = mybir.AxisListType

COMPUTE = ("pe", "act", "dve", "pool")
EPOCH = 30000
DMA_POOL = 10

D = 1024
SEQ = 2048
DEPTH = 4
HD = 64
NH = 24
MIXW = NH * HD
DFF = 2816
NFC = DFF // 128
EPS = 1e-6
DILS = (1, 4, 16)
BIG = 30000.0
FILLER = False


class Sched:
    def __init__(self, nc, same_engine_sync=True):
        self.nc = nc
        self.same_engine_sync = same_engine_sync
        self.handles = {"pe": nc.tensor, "act": nc.scalar, "dve": nc.vector,
                        "pool": nc.gpsimd, "sp": nc.sync}
        self.q = {e: [] for e in self.handles}
        self.sems = {e: [] for e in COMPUTE}
        self.cnt = {e: 0 for e in COMPUTE}
        self.pending = {e: ([], []) for e in COMPUTE}
        self.unsig = {e: set() for e in COMPUTE}
        self.dma_sems = {}
        self.dma_cnt = {}
        self.res = {}
        self.waited = {e: {} for e in self.handles}
        self.n_instr = 0
        self.defer = None

    def run_deferred(self, tasks, k=None):
        n = len(tasks) if k is None else min(k, len(tasks))
        saved, self.defer = self.defer, None
        for _ in range(n):
            kind, args = tasks.pop(0)
            (self.issue if kind == "i" else self.dma)(*args)
        self.defer = saved

    def _eng_event(self, eng):
        n = self.cnt[eng]
        ep = (n - 1) // EPOCH
        while len(self.sems[eng]) <= ep:
            self.sems[eng].append(self.nc.alloc_semaphore(f"s_{eng}_{len(self.sems[eng])}"))
        return (self.sems[eng][ep], n - ep * EPOCH, eng)

    def _need(self, eng, ev, waits, force=False):
        if ev is None:
            return
        sem, val, src = ev
        if src == eng and eng in COMPUTE and not force:
            if eng == "pe" or not self.same_engine_sync:
                return
        w = self.waited[eng]
        k = id(sem)
        if w.get(k, (None, 0))[1] >= val:
            return
        w[k] = (sem, val)
        waits[k] = (sem, max(val, waits.get(k, (sem, 0))[1]))

    def _check_unsig(self, eng, key):
        for e in COMPUTE:
            if e != eng and key in self.unsig[e]:
                raise RuntimeError(f"resource {key} has unsignaled access on {e}, needed by {eng}")

    def _deps(self, eng, reads, writes, force):
        waits = {}
        for key in reads:
            self._check_unsig(eng, key)
            st = self.res.get(key)
            if st is not None:
                self._need(eng, st[0], waits, force)
        for key in writes:
            self._check_unsig(eng, key)
            st = self.res.get(key)
            if st is not None:
                for ev in [st[0]] + st[1]:
                    if ev is not None and ev[2] == eng and eng in COMPUTE and not force:
                        continue
                    self._need(eng, ev, waits, force)
        return waits

    def issue(self, eng, fn, reads=(), writes=(), signal=True):
        if self.defer is not None:
            self.defer.append(("i", (eng, fn, list(reads), list(writes), signal)))
            return
        waits = self._deps(eng, reads, writes, False)
        inc = None
        pr, pw = self.pending[eng]
        pr.extend(reads)
        pw.extend(writes)
        if signal:
            self.cnt[eng] += 1
            ev = self._eng_event(eng)
            inc = (ev[0], 1)
            self._commit(pr, pw, ev)
            self.pending[eng] = ([], [])
            self.unsig[eng] = set()
        else:
            self.unsig[eng].update(reads)
            self.unsig[eng].update(writes)
        self.q[eng].append((fn, list(waits.values()), inc))
        self.n_instr += 1

    def _commit(self, reads, writes, ev):
        for key in reads:
            st = self.res.setdefault(key, [None, []])
            st[1].append(ev)
        for key in writes:
            self.res[key] = [ev, []]

    def dma(self, queue, fn, reads=(), writes=()):
        if self.defer is not None:
            self.defer.append(("d", (queue, fn, list(reads), list(writes))))
            return
        eng = queue
        if queue not in self.dma_sems:
            self.dma_sems[queue] = [self.nc.alloc_semaphore(f"d_{queue}_{i}") for i in range(DMA_POOL)]
            self.dma_cnt[queue] = 0
        waits = self._deps(eng, reads, writes, True)
        i = self.dma_cnt[queue]
        self.dma_cnt[queue] += 1
        sem = self.dma_sems[queue][i % DMA_POOL]
        rnd = i // DMA_POOL
        if rnd > 0:
            self._need(eng, (sem, 16 * rnd, "dma_" + queue), waits)
        ev = (sem, 16 * (rnd + 1), "dma_" + queue)
        self._commit(list(reads), list(writes), ev)
        self.q[eng].append((fn, list(waits.values()), (sem, 16)))
        self.n_instr += 1

    def barrier(self):
        evs = []
        for e in COMPUTE:
            assert not self.pending[e][0] and not self.pending[e][1], f"unsignaled tail on {e}"
            if self.cnt[e] > 0:
                evs.append(self._eng_event(e))
        for qn, sems in self.dma_sems.items():
            n = self.dma_cnt[qn]
            for slot, sem in enumerate(sems):
                uses = (n - slot + DMA_POOL - 1) // DMA_POOL if n > slot else 0
                if uses > 0:
                    evs.append((sem, 16 * uses, "dma_" + qn))
        for eng in self.handles:
            waits = {}
            for ev in evs:
                self._need(eng, ev, waits, False if ev[2] == eng else True)
            if waits:
                self.q[eng].append((None, list(waits.values()), None))

    def emit(self):
        nc = self.nc
        for e in COMPUTE:
            assert not self.pending[e][0] and not self.pending[e][1], f"unsignaled tail on {e}"
        with nc.Block() as block:
            def mk(eng):
                def body(h):
                    for fn, waits, inc in self.q[eng]:
                        for sem, val in waits:
                            h.wait_ge(sem, val)
                        if fn is None:
                            continue
                        ins = fn(h)
                        if inc is not None:
                            ins.then_inc(inc[0], inc[1])
                return body
            block.tensor(mk("pe"))
            block.scalar(mk("act"))
            block.vector(mk("dve"))
            block.gpsimd(mk("pool"))
            block.sync(mk("sp"))


CB_IDENT = 0
CB_ONES = 128
CB_TRIU_INCL = 256
CB_TRIU_STRICT = 384
CB_BAND2 = 512
CB_NEGU = 768
CB_NEGONES = 896
CB_ROPEC = 1024
CB_ROPES = 1024 + SEQ
CB_E8 = 1024 + 2 * SEQ
CB_W = 1024 + 3 * SEQ
CF_PB = 0
CF_PASTM = 128
CF_OWN = 256
CF_ONES = 384
CF_EPS = 392
CF_M0 = 393
CF_M1 = 394
CF_SW = 400
CF_IDENT = 528
CF_W = 656


def _host_consts():
    cb = np.zeros((128, CB_W), np.float32)
    s = np.arange(128)[:, None]
    t = np.arange(128)[None, :]
    cb[:, CB_IDENT:CB_IDENT + 128] = (s == t)
    cb[:, CB_ONES:CB_ONES + 128] = 1.0
    cb[:, CB_TRIU_INCL:CB_TRIU_INCL + 128] = np.where(s <= t, 0.0, -BIG)
    cb[:, CB_TRIU_STRICT:CB_TRIU_STRICT + 128] = np.where(s < t, 0.0, -BIG)
    t2 = np.arange(256)[None, :]
    cb[:, CB_BAND2:CB_BAND2 + 256] = np.where(((t2 - s) >= 0) & ((t2 - s) <= 128), 0.0, -BIG)
    cb[:, CB_NEGU:CB_NEGU + 128] = -1.0 * (s >= t)
    cb[:, CB_NEGONES:CB_NEGONES + 128] = -1.0
    pos = np.arange(SEQ, dtype=np.float32)
    inv = (np.float32(10000.0) ** (-np.arange(0, HD, 2, dtype=np.float32) / np.float32(HD))).astype(np.float32)
    ang = (pos[None, :] * inv[:, None]).astype(np.float32)
    cos = np.cos(ang).astype(np.float32)
    sin = np.sin(ang).astype(np.float32)
    for p in range(128):
        d = p % 64
        cb[p, CB_ROPEC:CB_ROPEC + SEQ] = cos[d % 32]
        cb[p, CB_ROPES:CB_ROPES + SEQ] = (-sin[d % 32]) if d < 32 else sin[d % 32]
    for n in range(8):
        cb[64 + n, CB_E8 + n * 256: CB_E8 + (n + 1) * 256] = 1.0
    cf = np.zeros((128, CF_W), np.float32)
    for tt in range(16):
        own = tt // 2
        for n in range(8):
            cf[:, CF_PB + tt * 8 + n] = 0.0 if n < own else -1e30
            cf[:, CF_PASTM + tt * 8 + n] = 1.0 if n < own else 0.0
            cf[:, CF_OWN + tt * 8 + n] = 1.0 if n == own else 0.0
    cf[:, CF_ONES:CF_ONES + 8] = 1.0
    cf[:, CF_EPS] = EPS
    cf[:64, CF_M0] = 1.0
    cf[64:, CF_M1] = 1.0
    for k in range(128):
        cf[k, CF_SW + (k + 64) % 128] = 1.0
        cf[k, CF_IDENT + k] = 1.0
    return cb, cf


SB_BASE = 16512
SB_TOP = 229344


def build(n_layers=DEPTH, stop_after=None, dbg=False):
    nc = bass.Bass("TRN2", target_bir_lowering=False)
    ext_in = lambda name, shape: nc.dram_tensor(name, list(shape), F32, kind="ExternalInput").ap()
    xT_d = ext_in("xT", (D, SEQ))
    c_d = ext_in("c", (128, 8))
    w_ada = ext_in("w_ada", (DEPTH, D, 6 * D))
    b_ada = ext_in("b_ada", (DEPTH, 6 * D))
    norm1_g = ext_in("norm1_g", (DEPTH, D))
    w_in = ext_in("w_in", (DEPTH, D, 3 * MIXW))
    w_br_a = ext_in("w_br_a", (DEPTH, 256, D))
    w_br_b = ext_in("w_br_b", (DEPTH, 384, D))
    w_br_c = ext_in("w_br_c", (DEPTH, 384, D))
    w_gate = ext_in("w_gate", (DEPTH, D, 3 * D))
    b_gate = ext_in("b_gate", (DEPTH, 3 * D))
    w_out = ext_in("w_out", (DEPTH, D, D))
    norm2_g = ext_in("norm2_g", (DEPTH, D))
    w_gu = ext_in("w_gu", (DEPTH, D, 2 * DFF))
    w_down = ext_in("w_down", (DEPTH, DFF, D))
    final_g = ext_in("final_g", (1, D))
    cb_d = ext_in("cb", (128, CB_W))
    cf_d = ext_in("cf", (128, CF_W))
    outT_d = nc.dram_tensor("outT", [D, SEQ], F32, kind="ExternalOutput").ap()
    skind = "ExternalOutput" if dbg else "Internal"
    qT_s = nc.dram_tensor("qT_s", [MIXW, SEQ], BF16, kind=skind).ap()
    kT_s = nc.dram_tensor("kT_s", [MIXW, SEQ], BF16, kind=skind).ap()
    v_s = nc.dram_tensor("v_s", [SEQ, MIXW], BF16, kind=skind).ap()
    g_s = nc.dram_tensor("g_s", [3 * D, SEQ], BF16, kind=skind).ap()
    if dbg:
        hT_dbg = nc.dram_tensor("hT_dbg", [D, SEQ], BF16, kind="ExternalOutput").ap()
        oT_dbg = nc.dram_tensor("oT_dbg", [D, SEQ], BF16, kind="ExternalOutput").ap()
        x1_dbg = nc.dram_tensor("x1_dbg", [D, SEQ], F32, kind="ExternalOutput").ap()
        mod_dbg = nc.dram_tensor("mod_dbg", [128, 48], F32, kind="ExternalOutput").ap()

    S = Sched(nc)

    off = [SB_BASE]

    def palloc(name, shape, dt, nbytes):
        t = nc.alloc_sbuf_tensor_at(name, list(shape), dt, offset=off[0])
        off[0] += (nbytes + 63) // 64 * 64
        return t

    xT = palloc("xT", [128, 8, SEQ], F32, 8 * SEQ * 4)
    cb = palloc("cbt", [128, CB_E8], BF16, CB_E8 * 2)
    cf = palloc("cft", [128, CF_W], F32, CF_W * 4)
    modT = palloc("modT", [128, DEPTH, 48], F32, DEPTH * 48 * 4)
    gT = palloc("gT", [128, DEPTH, 16], F32, DEPTH * 16 * 4)
    bgT = palloc("bgT", [128, DEPTH, 24], F32, DEPTH * 24 * 4)
    fgT = palloc("fgT", [128, 8], F32, 32)
    AB = palloc("AB", [128, DEPTH, 16], F32, DEPTH * 16 * 4)
    cin = palloc("cin", [128, 8], F32, 32)
    cact = palloc("cact", [128, 8], F32, 32)
    ARENA = off[0]
    assert ARENA < SB_TOP

    acache = {}

    def alloc_at(name, shape, dt, offset):
        key = (name, offset, tuple(shape), str(dt))
        if key not in acache:
            acache[key] = nc.alloc_sbuf_tensor_at(name, list(shape), dt, offset=offset)
        return acache[key]

    def arena():
        o = [ARENA]

        def a(name, shape, dt, nbytes):
            t = alloc_at(name, shape, dt, o[0])
            o[0] += (nbytes + 63) // 64 * 64
            assert o[0] <= SB_TOP, (name, o[0], SB_TOP)
            return t
        a.o = o
        return a

    ps = [nc.alloc_psum_tensor(f"ps{i}", [128, 512], F32) for i in range(8)]
    PK = lambda i: ("ps", i)

    ident = cb[:, CB_IDENT:CB_IDENT + 128]
    ones_bf = cb[:, CB_ONES:CB_ONES + 128]
    triu_incl = cb[:, CB_TRIU_INCL:CB_TRIU_INCL + 128]
    triu_strict = cb[:, CB_TRIU_STRICT:CB_TRIU_STRICT + 128]
    band2 = cb[:, CB_BAND2:CB_BAND2 + 256]
    negU = cb[:, CB_NEGU:CB_NEGU + 128]
    negones = cb[:, CB_NEGONES:CB_NEGONES + 128]
    ropeC = cb[:, CB_ROPEC:CB_ROPEC + SEQ]
    ropeS = cb[:, CB_ROPES:CB_ROPES + SEQ]
    ones_f = cf[:, CF_ONES:CF_ONES + 8]
    eps_c = cf[:, CF_EPS:CF_EPS + 1]
    half_m = (cf[:, CF_M0:CF_M0 + 1], cf[:, CF_M1:CF_M1 + 1])
    swap_f = cf[:, CF_SW:CF_SW + 128]
    ident_f = cf[:, CF_IDENT:CF_IDENT + 128]

    def mm(out, lhsT, rhs, start, stop, reads, writes, signal, skip=False):
        S.issue("pe", lambda h, o=out, l=lhsT, r=rhs, a=start, b=stop, sk=skip:
                h.matmul(o, lhsT=l, rhs=r, start=a, stop=b, skip_group_check=sk),
                reads, writes, signal)

    def act(out, in_, func, reads, writes, scale=None, bias=None):
        kw = {}
        if scale is not None:
            kw["scale"] = scale
        if bias is not None:
            kw["bias"] = bias
        S.issue("act", lambda h, o=out, i=in_, f=func, kw=kw: h.activation(out=o, in_=i, func=f, **kw),
                reads, writes)

    def tt(eng, out, in0, in1, op, reads, writes):
        S.issue(eng, lambda h, o=out, a=in0, b=in1, p=op: h.tensor_tensor(out=o, in0=a, in1=b, op=p),
                reads, writes)

    def ts(eng, out, in0, s1, op0, reads, writes, s2=None, op1=None):
        if op1 is None:
            S.issue(eng, lambda h, o=out, a=in0, s=s1, p=op0: h.tensor_scalar(out=o, in0=a, scalar1=s, scalar2=None, op0=p),
                    reads, writes)
        else:
            S.issue(eng, lambda h, o=out, a=in0, s=s1, p=op0, s2=s2, p1=op1:
                    h.tensor_scalar(out=o, in0=a, scalar1=s, scalar2=s2, op0=p, op1=p1), reads, writes)

    def stt(out, in0, scalar, in1, op0, op1, reads, writes):
        S.issue("dve", lambda h, o=out, a=in0, s=scalar, b=in1, p0=op0, p1=op1:
                h.scalar_tensor_tensor(out=o, in0=a, scalar=s, in1=b, op0=p0, op1=p1), reads, writes)

    def cp(eng, out, in_, reads, writes):
        if eng == "act":
            S.issue("act", lambda h, o=out, i=in_: h.copy(out=o, in_=i), reads, writes)
        else:
            S.issue(eng, lambda h, o=out, i=in_: h.tensor_copy(out=o, in_=i), reads, writes)

    def dma(queue, out, in_, reads, writes, slow=False):
        S.dma(queue, lambda h, o=out, i=in_, sl=slow: h.dma_start(out=o, in_=i, allow_slow_non_contiguous=sl),
              reads, writes)

    XK = lambda kc, n: ("xT", kc, n)
    XK_ALL = [XK(kc, n) for kc in range(8) for n in range(4)]

    for c0 in range(0, CB_E8, 1024):
        dma("pool", cb[:, c0:c0 + 1024], cb_d[:, c0:c0 + 1024], [], [("cbl", c0)])
    dma("sp", cf[:], cf_d, [], ["cf"])
    dma("sp", cin[:], c_d, [], ["cin"])
    for kc in range(8):
        dma("pool", xT[:, kc, :], xT_d[kc * 128:(kc + 1) * 128, :], [], [XK(kc, n) for n in range(4)])
    act(cact[:], cin[:], AF.Silu, ["cin"], ["cact"])

    A = arena()
    rows_g1 = [A(f"rg1_{l}", [1, D], F32, D * 4) for l in range(n_layers)]
    rows_g2 = [A(f"rg2_{l}", [1, D], F32, D * 4) for l in range(n_layers)]
    rows_bg = [A(f"rbg_{l}", [1, 3 * D], F32, 3 * D * 4) for l in range(n_layers)]
    rows_fg = A("rfg", [1, D], F32, D * 4)
    acc_h0 = A("acch", [128, 3 * D], F32, 3 * D * 4)
    wap0 = [A(f"wap{i}", [128, 512], F32, 2048) for i in range(8)]
    brow0 = A("brow", [48, 128], F32, 512)
    bT0 = A("bT", [128, 48], F32, 192)

    def row_to_cols(row_ap, ncols, bank, dst, dst_key, rkey):
        for j in range(ncols):
            mm(ps[bank][:, j:j + 1], row_ap[0:1, j * 128:(j + 1) * 128], ones_f[0:1, 0:1],
               True, True, [rkey, "cf"], [PK(bank)], j == ncols - 1)
        cp("act", dst, ps[bank][:, 0:ncols], [PK(bank)], [dst_key])

    def emit_adaln(l, acc_h, wap, brow, bT, dep_keys, b1, b2):
        idx = 0
        for half in range(2):
            for kc in range(8):
                for pcs in range(6):
                    c0 = half * 3072 + pcs * 512
                    nw = len(wap)
                    w = wap[idx % nw]
                    dma("sp", w[:], w_ada[l, kc * 128:(kc + 1) * 128, c0:c0 + 512], dep_keys if idx < nw else [],
                        [("wap", idx % nw)])
                    dst = acc_h[:, pcs * 512:(pcs + 1) * 512]
                    if kc == 0:
                        ts("dve", dst, w[:], cact[:, 0:1], ALU.mult, [("wap", idx % nw), "cact"] + list(dep_keys),
                           [("acch", pcs)])
                    else:
                        stt(dst, w[:], cact[:, kc:kc + 1], dst, ALU.mult, ALU.add,
                            [("wap", idx % nw), "cact", ("acch", pcs)], [("acch", pcs)])
                    idx += 1
            for jj in range(24):
                j = half * 24 + jj
                mm(ps[b1][:, j:j + 1], acc_h[:, jj * 128:(jj + 1) * 128], ones_f[:, 0:1], True, True,
                   [("acch", jj // 4), "cf"], [PK(b1)], jj == 23)
        dma("sp", brow[:, :], b_ada[l].rearrange("(a b) -> a b", b=128), list(dep_keys), ["brow"])
        S.issue("pe", lambda h: h.transpose(ps[b2][:, 0:48], brow[:, :], ident_f[0:48, 0:48]), ["brow", "cf"], [PK(b2)])
        cp("act", bT[:, :], ps[b2][:, 0:48], [PK(b2)], ["bT"])
        tt("dve", modT[:, l, :], ps[b1][:, 0:48], bT[:, :], ALU.add, [PK(b1), "bT"], [("modT", l)])
        for w_, sc0 in ((0, 8), (1, 32)):
            ts("dve", AB[:, l, w_ * 8:(w_ + 1) * 8], modT[:, l, sc0:sc0 + 8], 1.0, ALU.add,
               [("modT", l)], [("AB", l, w_)])
            tt("dve", AB[:, l, w_ * 8:(w_ + 1) * 8], AB[:, l, w_ * 8:(w_ + 1) * 8], gT[:, l, w_ * 8:(w_ + 1) * 8],
               ALU.mult, [("AB", l, w_), ("gT", l, w_)], [("AB", l, w_)])

    def small_vectors(l):
        dma("pool", rows_g1[l][0:1, :], norm1_g[l:l + 1, :], [], [("rg1", l)])
        dma("pool", rows_g2[l][0:1, :], norm2_g[l:l + 1, :], [], [("rg2", l)])
        dma("pool", rows_bg[l][0:1, :], b_gate[l:l + 1, :], [], [("rbg", l)])
        row_to_cols(rows_g1[l], 8, 1, gT[:, l, 0:8], ("gT", l, 0), ("rg1", l))
        row_to_cols(rows_g2[l], 8, 1, gT[:, l, 8:16], ("gT", l, 1), ("rg2", l))
        row_to_cols(rows_bg[l], 24, 1, bgT[:, l, :], ("bgT", l), ("rbg", l))
    small_vectors(0)
    emit_adaln(0, acc_h0, wap0, brow0, bT0, [], 2, 3)
    for l in range(1, n_layers):
        small_vectors(l)
    dma("pool", rows_fg[0:1, :], final_g[0:1, :], [], ["rfg"])
    row_to_cols(rows_fg, 8, 1, fgT[:, :], "fgT", "rfg")
    if dbg:
        dma("sp", mod_dbg, modT[:, 0, :], [("modT", 0)], ["mod_dbg"])
    S.barrier()

    def norm_phase(hT, scale_ap, bias_ap, skeys, A, tag, f32_out=None):
        sq = A(f"sq{tag}", [128, 8, 512], BF16, 8 * 512 * 2)
        rs = [A(f"rs{tag}{i}", [128, 512], F32, 2048) for i in range(2)]
        tmp = [A(f"tmp{tag}{i}", [128, 512], F32, 2048) for i in range(2)]
        BANK = 7
        for n in range(4):
            cols = slice(n * 512, (n + 1) * 512)
            for kc in range(8):
                act(sq[:, kc, :], xT[:, kc, cols], AF.Square, [XK(kc, n)], [("sq", kc)])
                mm(ps[BANK][:, :], ones_bf, sq[:, kc, :], kc == 0, kc == 7, [("sq", kc), "cb"], [PK(BANK)], kc == 7)
            r = rs[n % 2]
            act(r[:], ps[BANK][:, :], AF.Sqrt, [PK(BANK), "cf"], [("rs", n % 2)], scale=1.0 / D, bias=eps_c)
            S.issue("dve", lambda h, o=r: h.reciprocal(out=o[:], in_=o[:]), [("rs", n % 2)], [("rs", n % 2)])
            for kc in range(8):
                t_ = tmp[kc % 2]
                tt("dve", t_[:], xT[:, kc, cols], r[:], ALU.mult, [XK(kc, n), ("rs", n % 2)], [("tmp", kc % 2)])
                if f32_out is None:
                    if bias_ap is not None:
                        act(hT[:, kc, cols], t_[:], AF.Identity, [("tmp", kc % 2)] + skeys, [("hT", kc, n)],
                            scale=scale_ap[:, kc:kc + 1], bias=bias_ap[:, kc:kc + 1])
                    else:
                        act(hT[:, kc, cols], t_[:], AF.Identity, [("tmp", kc % 2)] + skeys, [("hT", kc, n)],
                            scale=scale_ap[:, kc:kc + 1])
                else:
                    act(xT[:, kc, cols], t_[:], AF.Identity, [("tmp", kc % 2)] + skeys, [XK(kc, n)],
                        scale=scale_ap[:, kc:kc + 1])
                    dma("sp", outT_d[kc * 128:(kc + 1) * 128, cols], xT[:, kc, cols], [XK(kc, n)], [("out", kc, n)])

    def rope(dst, qa, qb, npart, A_tmp, keys_in, key_out):
        t1, t2 = A_tmp
        pr = slice(0, npart)

        def add_(n):
            cols = slice(n * 512, (n + 1) * 512)
            tt("dve", dst[pr, cols], t1[n % 2][pr, :], t2[n % 2][pr, :], ALU.add,
               [("ropet1", n % 2), ("ropet2", n % 2)], [(key_out, n)])
        for n in range(4):
            cols = slice(n * 512, (n + 1) * 512)
            tt("dve", t1[n % 2][pr, :], qa[pr, cols], ropeC[pr, cols], ALU.mult,
               list(keys_in[0]) + ["cb"], [("ropet1", n % 2)])
            tt("dve", t2[n % 2][pr, :], qb[pr, cols], ropeS[pr, cols], ALU.mult,
               list(keys_in[1]) + ["cb"], [("ropet2", n % 2)])
            if n > 0:
                add_(n - 1)
        add_(3)

    def load_swapped(dst, src_rows, nhalf, key):
        keys = []
        for hh in range(nhalf):
            b = hh * 64
            dma("sp", dst[b:b + 32, :], src_rows[b + 32:b + 64, :], [], [(key, 2 * hh)])
            dma("sp", dst[b + 32:b + 64, :], src_rows[b:b + 32, :], [], [(key, 2 * hh + 1)])
            keys += [(key, 2 * hh), (key, 2 * hh + 1)]
        return keys

    for l in range(n_layers):
        A = arena()
        hT = A("hT", [128, 8, SEQ], BF16, 8 * SEQ * 2)
        wb = [A(f"wb{i}", [128, 8, 512], BF16, 8 * 512 * 2) for i in range(3)]
        stq = [A(f"stq{i}", [128, SEQ], BF16, SEQ * 2) for i in range(2)]
        stv = [A(f"stv{i}", [128, 512], BF16, 1024) for i in range(2)]
        norm_phase(hT, AB[:, l, 0:8], modT[:, l, 0:8], [("AB", l, 0), ("modT", l)], A, "a")
        HK_ALL = [("hT", kc, n) for kc in range(8) for n in range(4)]
        if dbg and l == 0:
            for kc in range(8):
                dma("sp", hT_dbg[kc * 128:(kc + 1) * 128, :], hT[:, kc, :], [("hT", kc, n) for n in range(4)], ["hT_dbg"])
        w_in_v = w_in[l].rearrange("(kc p) c -> p kc c", p=128)
        w_gate_v = w_gate[l].rearrange("(kc p) c -> p kc c", p=128)
        groups = [("in", g) for g in range(9)] + [("gate", g) for g in range(6)]
        bank_i = [0]
        ev_i = [0]

        def load_group(gi):
            kind, g = groups[gi]
            src = (w_in_v if kind == "in" else w_gate_v)[:, :, g * 512:(g + 1) * 512]
            dma("pool", wb[gi % 3][:], src, [], [("wb", gi % 3)])
        load_group(0)
        load_group(1)
        for gi, (kind, g) in enumerate(groups):
            if gi + 2 < len(groups):
                load_group(gi + 2)
            wt = wb[gi % 3]
            wk = ("wb", gi % 3)
            if kind == "in" and g >= 6:
                for tt_ in range(16):
                    b = bank_i[0] % 4
                    bank_i[0] += 1
                    for kc in range(8):
                        mm(ps[b][:, :], hT[:, kc, tt_ * 128:(tt_ + 1) * 128], wt[:, kc, :], kc == 0, kc == 7,
                           [("hT", kc, tt_ // 4), wk], [PK(b)], kc == 7)
                    sv = stv[tt_ % 2]
                    if ev_i[0] % 2 == 0:
                        cp("act", sv[:], ps[b][:, :], [PK(b)], [("stv", tt_ % 2)])
                    else:
                        cp("dve", sv[:], ps[b][:, :], [PK(b)], [("stv", tt_ % 2)])
                    ev_i[0] += 1
                    dma("sp", v_s[tt_ * 128:(tt_ + 1) * 128, (g - 6) * 512:(g - 5) * 512], sv[:],
                        [("stv", tt_ % 2)], [("v_s", g - 6, tt_)])
            else:
                for cc in range(4):
                    fc = g * 4 + cc
                    st = stq[fc % 2]
                    sk = ("stq", fc % 2)
                    for n in range(4):
                        b = bank_i[0] % 4
                        bank_i[0] += 1
                        cols = slice(n * 512, (n + 1) * 512)
                        for kc in range(8):
                            mm(ps[b][:, :], wt[:, kc, cc * 128:(cc + 1) * 128], hT[:, kc, cols], kc == 0, kc == 7,
                               [("hT", kc, n), wk], [PK(b)], kc == 7)
                        if kind == "gate":
                            act(st[:, cols], ps[b][:, :], AF.Sigmoid, [PK(b), ("bgT", l)], [sk],
                                bias=bgT[:, l, fc:fc + 1])
                        else:
                            sc = 0.125 if fc < 12 else 1.0
                            if ev_i[0] % 2 == 0:
                                act(st[:, cols], ps[b][:, :], AF.Copy, [PK(b)], [sk], scale=sc)
                            else:
                                ts("dve", st[:, cols], ps[b][:, :], sc, ALU.mult, [PK(b)], [sk])
                            ev_i[0] += 1
                    if kind == "gate":
                        dma("sp", g_s[fc * 128:(fc + 1) * 128, :], st[:], [sk], [("g_s", fc)])
                    elif fc < 12:
                        dma("sp", qT_s[fc * 128:(fc + 1) * 128, :], st[:], [sk], [("qT_s", fc)])
                    else:
                        dma("sp", kT_s[(fc - 12) * 128:(fc - 11) * 128, :], st[:], [sk], [("kT_s", fc - 12)])
        S.barrier()
        if stop_after == "bc":
            break

        v_nat = v_s.rearrange("(kt p) c -> p kt c", p=128)
        tasks = []

        def record(fn, *a):
            S.defer = tasks
            fn(*a)
            S.defer = None

        def pipeline(blocks, stages, drip=2):
            n = len(blocks)
            maxs = max(sk for _, sk in stages)
            for it in range(n + maxs):
                S.run_deferred(tasks, drip)
                for fn, sk in stages:
                    i = it - sk
                    if 0 <= i < n:
                        fn(blocks[i], i)
            S.run_deferred(tasks)

        def dump_oT():
            for kc in range(8):
                dma("sp", oT_dbg[kc * 128:(kc + 1) * 128, :], oT[:, kc, :], [], ["oT_dbg"])

        A = arena()
        oT = A("oT", [128, 8, SEQ], BF16, 8 * SEQ * 2)
        vtm = A("vtm", [128, 16, 3, 192], BF16, 16 * 3 * 192 * 2)
        M_Pb = [A(f"Pbm{i}", [128, 512], BF16, 1024) for i in range(4)]
        M_Rt = [A(f"Rtm{i}", [128, 512], F32, 2048) for i in range(2)]
        M_Rs = [A(f"Rsm{i}", [128, 512], F32, 2048) for i in range(2)]
        SB_ARENA = 83968
        assert A.o[0] <= ARENA + SB_ARENA
        A.o[0] = ARENA + SB_ARENA
        M_qa = A("qa", [128, SEQ], BF16, SEQ * 2)
        M_qb = A("qb", [128, SEQ], BF16, SEQ * 2)
        M_ka = A("ka", [128, SEQ], BF16, SEQ * 2)
        M_kb = A("kb", [128, SEQ], BF16, SEQ * 2)
        M_qrs = [A(f"qg{i}", [128, SEQ], BF16, SEQ * 2) for i in range(2)]
        M_krs = [A(f"kg{i}", [128, SEQ], BF16, SEQ * 2) for i in range(2)]
        M_rt1 = [A(f"rt1m{i}", [128, 512], BF16, 1024) for i in range(2)]
        M_rt2 = [A(f"rt2m{i}", [128, 512], BF16, 1024) for i in range(2)]
        M_btin = A("btin", [128, 16, 72], BF16, 16 * 72 * 2)
        M_cmpb = A("cmpb", [128, 16, 8, 8], F32, 4096)
        M_gm = A("gm", [128, 16, 8], F32, 512)
        M_rank = A("rank", [128, 16, 8], F32, 512)
        M_kmf = A("kmf", [64, 8], F32, 32)
        M_kmb = A("kmb", [64, 8], BF16, 16)
        for i_ in range(2):
            for c0 in range(0, SEQ, 1024):
                dma("pool", M_krs[i_][64:72, c0:c0 + 1024], cb_d[64:72, CB_E8 + c0:CB_E8 + c0 + 1024], [], [("kaug_e", i_, c0)])
        S.issue("pool", lambda h: h.memset(M_btin[:], 0.0), [], ["btin"])
        PBt = cf[:, CF_PB:CF_PB + 128].rearrange("p (a b) -> p a b", b=8)
        PASTMt = cf[:, CF_PASTM:CF_PASTM + 128].rearrange("p (a b) -> p a b", b=8)
        OWNt = cf[:, CF_OWN:CF_OWN + 128].rearrange("p (a b) -> p a b", b=8)

        def moba_prep(hm):
            sl = hm % 2
            qaug, kaug = M_qrs[sl], M_krs[sl]
            fr = 768 + hm * 64
            dma("sp", M_qa[0:64, :], qT_s[fr:fr + 64, :], [], ["qa"])
            qbk = load_swapped(M_qb, qT_s[fr:fr + 64, :], 1, "qb")
            dma("sp", M_ka[0:64, :], kT_s[fr:fr + 64, :], [], ["ka"])
            kbk = load_swapped(M_kb, kT_s[fr:fr + 64, :], 1, "kb")
            rope(qaug, M_qa, M_qb, 64, (M_rt1, M_rt2), [["qa"], qbk], ("qaug", sl))
            rope(kaug, M_ka, M_kb, 64, (M_rt1, M_rt2), [["ka"], kbk], ("kaug", sl))
            QK_ = [(("qaug", sl), n) for n in range(4)]
            KK_ = [(("kaug", sl), n) for n in range(4)]
            S.issue("dve", lambda h, ka_=kaug: h.tensor_reduce(out=M_kmf[:, :], in_=ka_[0:64, :].rearrange("p (n s) -> p n s", s=256),
                                                                axis=AX.X, op=ALU.add), KK_, ["kmf"])
            ts("dve", M_kmb[:, :], M_kmf[:, :], 1.0 / 256.0, ALU.mult, ["kmf"], ["kmb"])
            for t_ in range(16):
                mm(ps[6][:, t_ * 8:(t_ + 1) * 8], qaug[0:64, t_ * 128:(t_ + 1) * 128], M_kmb[:, :], True, True,
                   QK_ + ["kmb"], [PK(6)], t_ == 15)
            gps = ps[6][:, 0:128].rearrange("p (a b) -> p a b", b=8)
            tt("dve", M_gm[:], gps, PBt, ALU.add, [PK(6), "cf"], ["gm"])
            tt("dve", M_cmpb[:], M_gm[:].unsqueeze(2).to_broadcast([128, 16, 8, 8]),
               M_gm[:].unsqueeze(3).to_broadcast([128, 16, 8, 8]), ALU.is_gt, ["gm"], ["cmpb"])
            S.issue("dve", lambda h: h.tensor_reduce(out=M_rank[:], in_=M_cmpb[:], axis=AX.X, op=ALU.add), ["cmpb"], ["rank"])
            ts("dve", M_rank[:], M_rank[:], 2.5, ALU.is_lt, ["rank"], ["rank"])
            tt("dve", M_rank[:], M_rank[:], PASTMt, ALU.mult, ["rank", "cf"], ["rank"])
            tt("dve", M_rank[:], M_rank[:], OWNt, ALU.add, ["rank", "cf"], ["rank"])
            ts("dve", M_btin[:, :, 64:72], M_rank[:], -1.0, ALU.add, ["rank"], ["btin"], s2=BIG, op1=ALU.mult)
            tps = ps[7][:, :].bitcast(BF16)
            for half in range(2):
                for t_ in range(8):
                    tq = half * 8 + t_
                    S.issue("pe", lambda h, o=tps[0:72, t_ * 128:(t_ + 1) * 128], i=M_btin[:, tq, :]:
                            h.transpose(o, i, ident), ["btin", "cb"], [PK(7)], t_ == 7)
                cp("act", qaug[64:72, half * 1024:(half + 1) * 1024], tps[64:72, 0:1024], [PK(7)], [("qaug_b", sl, half)])
        M_blocks = []
        M_gi = 0
        for hm in range(6):
            for qc in range(4):
                last_kt = 4 * qc + 3
                for kt in range(last_kt, -1, -1):
                    diag = kt >= 4 * qc
                    o_ = (kt - 4 * qc) * 128 if diag else 0
                    M_blocks.append(dict(hm=hm, qc=qc, kt=kt, diag=diag, o=o_, first=kt == last_kt, last=kt == 0, g=M_gi,
                                       newhead=(qc == 0 and kt == last_kt)))
                M_gi += 1

        def mb_s1(b, i):
            hm = b["hm"]
            if b["newhead"]:
                S.run_deferred(tasks)
                if hm + 1 < 6:
                    record(moba_prep, hm + 1)
            sl = hm % 2
            o_ = b["o"]
            qcols = slice(b["qc"] * 512 + o_, (b["qc"] + 1) * 512)
            kcols = slice(b["kt"] * 128, (b["kt"] + 1) * 128)
            deps = [(("qaug", sl), n) for n in range(4)] + [(("kaug", sl), n) for n in range(4)] + \
                   [("qaug_b", sl, 0), ("qaug_b", sl, 1), ("kaug_e", sl, 0), ("kaug_e", sl, 1024)]
            if b["diag"]:
                mm(ps[i % 2][:, o_:o_ + 128], ident, triu_incl, True, False, ["cb"], [PK(i % 2)], False, skip=True)
            mm(ps[i % 2][:, o_:512], M_krs[sl][0:72, kcols], M_qrs[sl][0:72, qcols], not b["diag"], True, deps, [PK(i % 2)],
               True, skip=True)

        def mb_s2(b, i):
            o_ = b["o"]
            act(M_Pb[i % 4][:, o_:512], ps[i % 2][:, o_:512], AF.Exp, [PK(i % 2)], [("Pb", i % 4)])

        def mb_s3(b, i):
            hm = b["hm"]
            pc, hf = divmod(hm, 2)
            pr = slice(hf * 64, hf * 64 + 64)
            po = slice((1 - hf) * 64, (1 - hf) * 64 + 64)
            o_ = b["o"]
            bn = 2 + b["g"] % 2
            mm(ps[bn][:, o_:512], vtm[:, b["kt"], pc, hf * 64:hf * 64 + 128], M_Pb[i % 4][:, o_:512], b["first"], b["last"],
               [("vtm", pc, 0), ("vtm", pc, 1), "vtm1", ("Pb", i % 4)], [PK(bn)], True, skip=True)

        def mb_s4(b, i):
            if not b["last"]:
                return
            hf = b["hm"] % 2
            po = slice((1 - hf) * 64, (1 - hf) * 64 + 64)
            bn = 2 + b["g"] % 2
            g2 = b["g"] % 2
            R = M_Rt[g2]
            act(R[po, :], ps[bn][po, :], AF.Ln, [PK(bn)], [("Rt", g2)])

        def mb_s4b(b, i):
            if not b["last"]:
                return
            hf = b["hm"] % 2
            po = slice((1 - hf) * 64, (1 - hf) * 64 + 64)
            g2 = b["g"] % 2
            R = M_Rt[g2]
            act(R[po, :], R[po, :], AF.Exp, [("Rt", g2)], [("Rt", g2)], scale=-1.0)

        def mb_s5(b, i):
            if not b["last"]:
                return
            g2 = b["g"] % 2
            mm(ps[4 + g2][:, :], swap_f, M_Rt[g2][:, :], True, True, [("Rt", g2), "cf"], [PK(4 + g2)], True)

        def mb_s6(b, i):
            if not b["last"]:
                return
            pc, hf = divmod(b["hm"], 2)
            pr = slice(hf * 64, hf * 64 + 64)
            bn = 2 + b["g"] % 2
            g2 = b["g"] % 2
            cols = slice(b["qc"] * 512, (b["qc"] + 1) * 512)
            cp("act", M_Rs[g2][pr, :], ps[4 + g2][pr, :], [PK(4 + g2)], [("Rs", g2)])
            tt("dve", oT[pr, 2 + pc, cols], ps[bn][pr, :], M_Rs[g2][pr, :], ALU.mult, [PK(bn), ("Rs", g2)],
               [("oT", 2 + pc, b["qc"])])

        A = arena()
        oT = A("oT", [128, 8, SEQ], BF16, 8 * SEQ * 2)
        vt = A("vt", [128, 16, 384], BF16, 16 * 384 * 2)
        QS = [A(f"sbq{i}", [128, SEQ], BF16, SEQ * 2) for i in range(2)]
        KZ = [[A(f"sbk{i}{h}", [128, SEQ], BF16, SEQ * 2) for h in range(2)] for i in range(2)]
        Pb = [A(f"Pb{i}", [128, 512], BF16, 1024) for i in range(4)]
        Ef = [A(f"Ef{i}", [128, 512], BF16, 1024) for i in range(3)]
        SPb = [A(f"SPb{i}", [128, 512], BF16, 1024) for i in range(4)]
        SPsb = [A(f"SPsb{i}", [128, 512], BF16, 1024) for i in range(2)]
        assert A.o[0] <= ARENA + SB_ARENA, A.o[0] - ARENA
        dma("sp", vt[:, :, :], v_nat[:, :, 1152:1536], [], ["vt"])
        for i_ in range(2):
            for h_ in range(2):
                S.issue("pool", lambda h, t=KZ[i_][h_]: h.memset(t[:], 0.0), [], [("sbk", i_, h_)])

        def sb_load(pc):
            ch = 9 + pc
            sl = pc % 2
            dma("sp", QS[sl][:], qT_s[ch * 128:(ch + 1) * 128, :], [], [("sbq", sl)])
            for h_ in range(2):
                dma("sp", KZ[sl][h_][h_ * 64:(h_ + 1) * 64, :], kT_s[ch * 128 + h_ * 64:ch * 128 + (h_ + 1) * 64, :], [],
                    [("sbk", sl, h_)])
        blocks = []
        gi = 0
        for pc in range(3):
            for hf in range(2):
                for qc in range(4):
                    last_kt = 4 * qc + 3
                    for kt in range(last_kt, -1, -1):
                        diag = kt >= 4 * qc
                        o_ = (kt - 4 * qc) * 128 if diag else 0
                        o_pv = (kt + 1 - 4 * qc) * 128 if (kt + 1) >= 4 * qc else 0
                        blocks.append(dict(pc=pc, hf=hf, qc=qc, kt=kt, diag=diag, o=o_, opv=o_pv, first=kt == last_kt,
                                           last=kt == 0, g=gi, newpair=(hf == 0 and qc == 0 and kt == last_kt)))
                    gi += 1
        sb_load(0)

        def sb_qk(b, bank, stop, sig):
            o_ = b["o"]
            sl = b["pc"] % 2
            qcols = slice(b["qc"] * 512 + o_, (b["qc"] + 1) * 512)
            kcols = slice(b["kt"] * 128, (b["kt"] + 1) * 128)
            if b["diag"]:
                mm(ps[bank][:, o_:o_ + 128], ident, triu_strict, True, False, ["cb"], [PK(bank)], False, skip=True)
            mm(ps[bank][:, o_:512], KZ[sl][b["hf"]][:, kcols], QS[sl][:, qcols], not b["diag"], stop,
               [("sbq", sl), ("sbk", sl, b["hf"])], [PK(bank)], sig, skip=True)

        def sb_s1(b, i):
            if b["newpair"] and b["pc"] + 1 < 3:
                sb_load(b["pc"] + 1)
            sb_qk(b, i % 2, True, True)

        def sb_s2(b, i):
            o_ = b["o"]
            act(Ef[i % 3][:, o_:512], ps[i % 2][:, o_:512], AF.Exp, [PK(i % 2)], [("Ef", i % 3)])

        def sb_s3(b, i):
            o_ = b["o"]
            act(SPb[i % 4][:, o_:512], Ef[i % 3][:, o_:512], AF.Ln, [("Ef", i % 3)], [("SPb", i % 4)], bias=1.0)

        def sb_s45(b, i):
            o_, opv = b["o"], b["opv"]
            ba = 2 + i % 2
            sb_qk(b, ba, False, False)
            if not b["first"]:
                mm(ps[ba][:, opv:512], negones, SPsb[(i - 1) % 2][:, opv:512], False, False,
                   [("SPsb", (i - 1) % 2), "cb"], [PK(ba)], False, skip=True)
            mm(ps[ba][:, o_:512], negU, SPb[i % 4][:, o_:512], False, True, [("SPb", i % 4), "cb"], [PK(ba)],
               True, skip=True)
            if not b["last"]:
                new = SPsb[i % 2]
                if b["first"]:
                    cp("dve", new[:, o_:512], SPb[i % 4][:, o_:512], [("SPb", i % 4)], [("SPsb", i % 2)])
                else:
                    if opv > o_:
                        cp("dve", new[:, o_:opv], SPb[i % 4][:, o_:opv], [("SPb", i % 4)], [("SPsb", i % 2)])
                    tt("dve", new[:, opv:512], SPsb[(i - 1) % 2][:, opv:512], SPb[i % 4][:, opv:512], ALU.add,
                       [("SPsb", (i - 1) % 2), ("SPb", i % 4)], [("SPsb", i % 2)])

        def sb_s6(b, i):
            o_ = b["o"]
            act(Pb[i % 4][:, o_:512], ps[2 + i % 2][:, o_:512], AF.Exp, [PK(2 + i % 2)], [("Pb", i % 4)])

        def sb_s7(b, i):
            pr = slice(b["hf"] * 64, b["hf"] * 64 + 64)
            o_ = b["o"]
            bo = 4 + b["g"] % 2
            mm(ps[bo][:, o_:512], vt[:, b["kt"], b["pc"] * 128:(b["pc"] + 1) * 128], Pb[i % 4][:, o_:512],
               b["first"], b["last"], ["vt", ("Pb", i % 4)], [PK(bo)], True, skip=True)
            if b["last"]:
                cols = slice(b["qc"] * 512, (b["qc"] + 1) * 512)
                cp("dve", oT[pr, 5 + b["pc"], cols], ps[bo][pr, :], [PK(bo)], [("oT", 5 + b["pc"], b["qc"])])
        record(moba_prep, 0)
        pipeline(blocks, [(sb_s1, 0), (sb_s2, 1), (sb_s3, 2), (sb_s45, 3), (sb_s6, 4), (sb_s7, 5)], drip=1)
        S.barrier()
        if stop_after == "sb":
            dump_oT()
            break

        for p_ in range(3):
            for e_ in range(2):
                c0 = 768 + p_ * 128 + e_ * 64
                dma("sp", vtm[:, :, p_, e_ * 128:e_ * 128 + 64], v_nat[:, :, c0:c0 + 64], [], [("vtm", p_, e_)])
        S.issue("pool", lambda h: h.memset(vtm[:, :, :, 64:128], 1.0), [], ["vtm1"])
        for i_ in range(2):
            S.issue("pool", lambda h, t=M_Rt[i_]: h.memset(t[:], 1.0), [], [("Rt", i_)])
        pipeline(M_blocks, [(mb_s1, 0), (mb_s2, 1), (mb_s3, 2), (mb_s4, 3), (mb_s4b, 4), (mb_s5, 5), (mb_s6, 6)], drip=2)
        S.barrier()

        A = arena()
        oT = A("oT", [128, 8, SEQ], BF16, 8 * SEQ * 2)
        vt2 = [A(f"vtd{i}", [128, 16, 192], BF16, 16 * 192 * 2) for i in range(2)]
        qa = A("qa", [128, SEQ], BF16, SEQ * 2)
        qb = A("qb", [128, SEQ], BF16, SEQ * 2)
        ka = A("ka", [128, SEQ], BF16, SEQ * 2)
        kb = A("kb", [128, SEQ], BF16, SEQ * 2)
        qrs = [A(f"qr{i}", [128, SEQ], BF16, SEQ * 2) for i in range(2)]
        krs = [A(f"kr{i}", [128, SEQ], BF16, SEQ * 2) for i in range(2)]
        kz = [[A(f"kz{i}{h}", [128, SEQ], BF16, SEQ * 2) for h in range(2)] for i in range(2)]
        rt1 = [A(f"rt1{i}", [128, 512], BF16, 1024) for i in range(2)]
        rt2 = [A(f"rt2{i}", [128, 512], BF16, 1024) for i in range(2)]
        Pb = [A(f"Pbd{i}", [128, 512], BF16, 1024) for i in range(4)]
        ACC = [A(f"ACC{h}", [128, SEQ], F32, SEQ * 4) for h in range(2)]
        Rt = [A(f"Rt{i}", [128, 512], F32, 2048) for i in range(2)]
        Rs = [A(f"Rs{i}", [128, 512], F32, 2048) for i in range(2)]
        for i_ in range(2):
            S.issue("pool", lambda h, t=vt2[i_]: h.memset(t[:, :, 64:128], 1.0), [], [("vtd1", i_)])

        def dil_prep(sc, g, slot):
            dl = DILS[g]
            nb = SEQ // dl // 128
            ch = 2 * g + sc
            dma("sp", qa[:], qT_s[ch * 128:(ch + 1) * 128, :], [], ["qa"])
            qbk = load_swapped(qb, qT_s[ch * 128:(ch + 1) * 128, :], 2, "qb")
            dma("sp", ka[:], kT_s[ch * 128:(ch + 1) * 128, :], [], ["ka"])
            kbk = load_swapped(kb, kT_s[ch * 128:(ch + 1) * 128, :], 2, "kb")
            vc0 = (4 * g + 2 * sc) * 64
            for r in range(dl):
                for h_ in range(2):
                    src = v_s[r:SEQ:dl, vc0 + h_ * 64:vc0 + (h_ + 1) * 64].rearrange("(j b) c -> b j c", b=128)
                    dma("sp", vt2[slot][:, r * nb:(r + 1) * nb, h_ * 128:h_ * 128 + 64], src, [],
                        [("vtd", slot, r, h_)] + ([("vtdany", slot)] if (r == 0 and h_ == 0) else []))
            rope(qrs[slot], qa, qb, 128, (rt1, rt2), [["qa"], qbk], ("qr", slot))
            rope(krs[slot], ka, kb, 128, (rt1, rt2), [["ka"], kbk], ("kr", slot))
            for n in range(4):
                cols = slice(n * 512, (n + 1) * 512)
                for h_ in range(2):
                    act(kz[slot][h_][:, cols], krs[slot][:, cols], AF.Identity, [(("kr", slot), n), "cf"],
                        [("kz", slot, h_, n)], scale=half_m[h_])
        blocks = []
        units = [(sc, g) for sc in range(2) for g in range(3)]
        for ui, (sc, g) in enumerate(units):
            dl = DILS[g]
            L = SEQ // dl
            nb = L // 128
            for hf in range(2):
                for ti, (r, j) in enumerate([(r, j) for r in range(dl) for j in range(nb)]):
                    blocks.append(dict(sc=sc, g=g, hf=hf, r=r, j=j, dl=dl, L=L, nb=nb, ui=ui,
                                       newunit=(hf == 0 and ti == 0), lastunit=(g == 2 and hf == 1 and ti == 15)))
        dil_prep(0, 0, 0)

        def dl_s1(b, i):
            if b["newunit"]:
                S.run_deferred(tasks)
                if b["ui"] + 1 < len(units):
                    nsc, ng = units[b["ui"] + 1]
                    record(dil_prep, nsc, ng, (b["ui"] + 1) % 2)
            slot = b["ui"] % 2
            dl, j, r, nb = b["dl"], b["j"], b["r"], b["nb"]
            N = 256 if j < nb - 1 else 128
            st0 = r + dl * 128 * j
            kcols = slice(st0, st0 + dl * 127 + 1, dl)
            qcols = slice(st0, st0 + dl * (N - 1) + 1, dl)
            if FILLER:
                mm(ps[6][:, 0:512], ident, ropeC[:, 0:512], True, True, ["cb"], [PK(6)], False, skip=True)
            mm(ps[i % 2][:, 0:N], ident, band2[:, 0:N], True, False, ["cb"], [PK(i % 2)], False, skip=True)
            mm(ps[i % 2][:, 0:N], kz[slot][b["hf"]][:, kcols], qrs[slot][:, qcols], False, True,
               [("kz", slot, b["hf"], n) for n in range(4)] + [(("qr", slot), n) for n in range(4)], [PK(i % 2)],
               True, skip=True)

        def dl_s2(b, i):
            N = 256 if b["j"] < b["nb"] - 1 else 128
            act(Pb[i % 4][:, 0:N], ps[i % 2][:, 0:N], AF.Exp, [PK(i % 2)], [("Pb", i % 4)])

        def dl_s3(b, i):
            slot = b["ui"] % 2
            hf, g, sc = b["hf"], b["g"], b["sc"]
            dl, j, r, nb, L = b["dl"], b["j"], b["r"], b["nb"], b["L"]
            nblk = 2 if j < nb - 1 else 1
            u0 = r * L + 128 * j
            vi = r * nb + j
            for blk in range(nblk):
                u = u0 + 128 * blk
                q4 = u // 512
                col = u % 512
                bk = 2 + (q4 % 2)
                opening = (blk == 1) or (j == 0)
                closing = (blk == 0)
                mm(ps[bk][:, col:col + 128], vt2[slot][:, vi, hf * 64:hf * 64 + 128],
                   Pb[i % 4][:, blk * 128:(blk + 1) * 128], opening, closing,
                   [("vtd", slot, r, 0), ("vtd", slot, r, 1), ("vtdany", slot), ("vtd1", slot), ("Pb", i % 4)],
                   [PK(bk)], True, skip=True)
                if closing and col == 384:
                    accx = ACC[hf]
                    if g == 0:
                        cp("dve", accx[:, q4 * 512:(q4 + 1) * 512], ps[bk][:, :], [PK(bk)], [("ACC", hf)])
                    else:
                        if g == 1:
                            dst = accx[:, q4:SEQ:4]
                            src_ = ps[bk][:, :]
                        else:
                            dst = accx[:, :].rearrange("p (m r) -> p r m", r=16)[:, 4 * q4:4 * q4 + 4, :]
                            src_ = ps[bk][:, :].rearrange("p (r m) -> p r m", r=4)
                        tt("dve", dst, src_, dst, ALU.add, [PK(bk), ("ACC", hf)], [("ACC", hf)])
            if b["lastunit"] and j == nb - 1 and r == dl - 1:
                lo, hi = slice(0, 64), slice(64, 128)
                for n in range(4):
                    cols = slice(n * 512, (n + 1) * 512)
                    R = Rt[n % 2]
                    act(R[lo, :], ACC[1][lo, cols], AF.Ln, [("ACC", 1)], [("Rt", n % 2)])
                    act(R[hi, :], ACC[0][hi, cols], AF.Ln, [("ACC", 0)], [("Rt", n % 2)])
                    act(R[:, :], R[:, :], AF.Exp, [("Rt", n % 2)], [("Rt", n % 2)], scale=-1.0)
                    mm(ps[4 + n % 2][:, :], swap_f, R[:, :], True, True, [("Rt", n % 2), "cf"], [PK(4 + n % 2)], True)
                    cp("act", Rs[n % 2][:, :], ps[4 + n % 2][:, :], [PK(4 + n % 2)], [("Rs", n % 2)])
                    tt("dve", oT[lo, sc, cols], ACC[0][lo, cols], Rs[n % 2][lo, :], ALU.mult,
                       [("ACC", 0), ("Rs", n % 2)], [("oT", sc, n, 0)])
                    tt("pool", oT[hi, sc, cols], ACC[1][hi, cols], Rs[n % 2][hi, :], ALU.mult,
                       [("ACC", 1), ("Rs", n % 2)], [("oT", sc, n, 1)])
        pipeline(blocks, [(dl_s1, 0), (dl_s2, 1), (dl_s3, 2)], drip=2)
        S.barrier()
        if stop_after == "dil":
            dump_oT()
            break

        if dbg and l == 0:
            S.barrier()
            dump_oT()
        S.barrier()
        if stop_after == "attn":
            break

        A = arena()
        oT = A("oT", [128, 8, SEQ], BF16, 8 * SEQ * 2)
        wbr = A("wbr", [128, 8, D], BF16, 8 * D * 2)
        wout = A("wout", [128, 8, D], BF16, 8 * D * 2)
        gst = [A(f"gst{i}", [128, 3, 512], BF16, 3 * 512 * 2) for i in range(2)]
        mgs = [A(f"mg{i}", [128, 8, 512], BF16, 8 * 512 * 2) for i in range(2)]
        et = [A(f"et{i}", [128, 512], BF16, 1024) for i in range(9)]
        dma("pool", wbr[:, 0:2, :], w_br_a[l].rearrange("(kc p) c -> p kc c", p=128), [], [("wbr", 0)])
        dma("pool", wbr[:, 2:5, :], w_br_b[l].rearrange("(kc p) c -> p kc c", p=128), [], [("wbr", 1)])
        dma("pool", wbr[:, 5:8, :], w_br_c[l].rearrange("(kc p) c -> p kc c", p=128), [], [("wbr", 2)])
        dma("pool", wout[:, 0:4, :], w_out[l, 0:512, :].rearrange("(kc p) c -> p kc c", p=128), [], [("wout", 0)])
        dma("pool", wout[:, 4:8, :], w_out[l, 512:1024, :].rearrange("(kc p) c -> p kc c", p=128), [], [("wout", 1)])
        g_v = g_s.rearrange("(i j p) t -> j p i t", i=3, p=128)
        br_rng = ((0, 2), (2, 5), (5, 8))
        ei = [0]
        def outproj(n):
            cols = slice(n * 512, (n + 1) * 512)
            mg = mgs[n % 2]
            for j2 in range(8):
                b = 4 + (j2 % 2)
                for j in range(8):
                    mm(ps[b][:, :], wout[:, j, j2 * 128:(j2 + 1) * 128], mg[:, j, :], j == 0, j == 7,
                       [("wout", j // 4), ("mg", n % 2, j)], [PK(b)], j == 7)
                stt(xT[:, j2, cols], ps[b][:, :], modT[:, l, 16 + j2:17 + j2], xT[:, j2, cols], ALU.mult, ALU.add,
                    [PK(b), ("modT", l), XK(j2, n)], [XK(j2, n)])
        pend_add = [None]
        for n in range(4):
            cols = slice(n * 512, (n + 1) * 512)
            mg = mgs[n % 2]
            for j in range(8):
                gt = gst[j % 2]
                dma("sp", gt[:, :, :], g_v[j][:, :, cols], [("g_s", i * 8 + j) for i in range(3)], [("gst", j % 2)])
                bset = (0, 1, 2) if j % 2 == 0 else (3, 6, 7)
                for i, (k0, k1) in enumerate(br_rng):
                    for kc in range(k0, k1):
                        mm(ps[bset[i]][:, :], wbr[:, kc, j * 128:(j + 1) * 128], oT[:, kc, cols], kc == k0, kc == k1 - 1,
                           [("wbr", i), ("oT", kc, n)], [PK(bset[i])], kc == k1 - 1)
                i0 = 3 * ((n * 8 + j) % 3)
                e0, e1, e2 = et[i0], et[i0 + 1], et[i0 + 2]
                tt("dve", e0[:], ps[bset[0]][:, :], gt[:, 0, :], ALU.mult, [PK(bset[0]), ("gst", j % 2)], [("et", i0)])
                tt("dve", e1[:], ps[bset[1]][:, :], gt[:, 1, :], ALU.mult, [PK(bset[1]), ("gst", j % 2)], [("et", i0 + 1)])
                tt("dve", e2[:], ps[bset[2]][:, :], gt[:, 2, :], ALU.mult, [PK(bset[2]), ("gst", j % 2)], [("et", i0 + 2)])
                if pend_add[0] is not None:
                    pend_add[0]()
                pend_add[0] = (lambda e0=e0, e1=e1, e2=e2, i0=i0, mg=mg, j=j, n=n: (
                    tt("dve", e0[:], e0[:], e1[:], ALU.add, [("et", i0), ("et", i0 + 1)], [("et", i0)]),
                    tt("dve", mg[:, j, :], e0[:], e2[:], ALU.add, [("et", i0), ("et", i0 + 2)], [("mg", n % 2, j)])))
                if n > 0 and j == 4:
                    outproj(n - 1)
        pend_add[0]()
        outproj(3)
        if dbg and l == 0:
            for kc in range(8):
                dma("sp", x1_dbg[kc * 128:(kc + 1) * 128, :], xT[:, kc, :], [XK(kc, n) for n in range(4)], ["x1_dbg"])
        S.barrier()
        if stop_after == "mix":
            break

        A = arena()
        hT = A("hT", [128, 8, SEQ], BF16, 8 * SEQ * 2)
        actT = A("actT", [128, NFC, 1024], BF16, NFC * 1024 * 2)
        wgu = [A(f"wgu{i}", [128, 8, 512], BF16, 8 * 512 * 2) for i in range(2)]
        wdn = [A(f"wdn{i}", [128, NFC, 128], BF16, NFC * 128 * 2) for i in range(2)]
        sil = [A(f"sil{i}", [128, 512], F32, 2048) for i in range(2)]
        browf = A("browf", [48, 128], F32, 512)
        bTf = A("bTf", [128, 48], F32, 192)
        norm_off = A.o[0]
        norm_phase(hT, AB[:, l, 8:16], modT[:, l, 24:32], [("AB", l, 1), ("modT", l)], A, "b")
        atasks = []
        if l + 1 < n_layers:
            acc_hf = alloc_at("acchf", [128, 3 * D], F32, norm_off)
            wapf = [alloc_at(f"wapf{i}", [128, 512], F32, norm_off + 3 * D * 4 + i * 2048) for i in range(2)]
            assert norm_off + 3 * D * 4 + 4096 <= A.o[0]
            S.defer = atasks
            emit_adaln(l + 1, acc_hf, wapf, browf, bTf, [("hT", kc, 3) for kc in range(8)], 6, 7)
            S.defer = None
        w_gu_v = w_gu[l].rearrange("(kc p) c -> p kc c", p=128)
        w_dn_v = w_down[l].rearrange("(f p) c -> p f c", p=128)
        fi = [0]
        for half in range(2):
            def load_gu(fg):
                dma("pool", wgu[fg % 2][:, :, 0:256], w_gu_v[:, :, fg * 256:(fg + 1) * 256], [], [("wgu", fg % 2, 0)])
                dma("pool", wgu[fg % 2][:, :, 256:512], w_gu_v[:, :, DFF + fg * 256:DFF + (fg + 1) * 256], [], [("wgu", fg % 2, 1)])
            load_gu(0)
            for fg in range(11):
                if fg + 1 < 11:
                    load_gu(fg + 1)
                wt = wgu[fg % 2]
                for cc in range(2):
                    f = fg * 2 + cc
                    for nn in range(2):
                        n = half * 2 + nn
                        cols = slice(n * 512, (n + 1) * 512)
                        bg = (fi[0] % 2) * 2
                        bu = bg + 1
                        fi[0] += 1
                        for kc in range(8):
                            mm(ps[bg][:, :], wt[:, kc, cc * 128:(cc + 1) * 128], hT[:, kc, cols], kc == 0, kc == 7,
                               [("wgu", fg % 2, 0), ("hT", kc, n)], [PK(bg)], kc == 7)
                        for kc in range(8):
                            mm(ps[bu][:, :], wt[:, kc, 256 + cc * 128:256 + (cc + 1) * 128], hT[:, kc, cols], kc == 0, kc == 7,
                               [("wgu", fg % 2, 1), ("hT", kc, n)], [PK(bu)], kc == 7)
                        S.run_deferred(atasks, 2)
                        sl = sil[fi[0] % 2]
                        act(sl[:], ps[bg][:, :], AF.Silu, [PK(bg)], [("sil", fi[0] % 2)])
                        tt("dve", actT[:, f, nn * 512:(nn + 1) * 512], ps[bu][:, :], sl[:], ALU.mult,
                           [PK(bu), ("sil", fi[0] % 2)], [("actT", f, nn)])
            dma("pool", wdn[0][:], w_dn_v[:, :, 0:128], [], [("wdn", 0)])
            for j in range(8):
                if j + 1 < 8:
                    dma("pool", wdn[(j + 1) % 2][:], w_dn_v[:, :, (j + 1) * 128:(j + 2) * 128], [], [("wdn", (j + 1) % 2)])
                for nn in range(2):
                    n = half * 2 + nn
                    cols = slice(n * 512, (n + 1) * 512)
                    b = 4 + ((j * 2 + nn) % 2)
                    S.run_deferred(atasks, 3)
                    for f in range(NFC):
                        mm(ps[b][:, :], wdn[j % 2][:, f, :], actT[:, f, nn * 512:(nn + 1) * 512], f == 0, f == NFC - 1,
                           [("wdn", j % 2), ("actT", f, nn)], [PK(b)], f == NFC - 1)
                    stt(xT[:, j, cols], ps[b][:, :], modT[:, l, 40 + j:41 + j], xT[:, j, cols], ALU.mult, ALU.add,
                        [PK(b), ("modT", l), XK(j, n)], [XK(j, n)])
        S.run_deferred(atasks)
        S.barrier()

    if stop_after is None:
        A = arena()
        norm_phase(None, fgT, None, ["fgT"], A, "f", f32_out=True)
    else:
        for kc in range(8):
            dma("sp", outT_d[kc * 128:(kc + 1) * 128, :], xT[:, kc, :], [XK(kc, n) for n in range(4)], [("out", kc)])
    S.barrier()
    S.emit()
    return nc, S


_CACHE = {}


def _prep_inputs(inputs):
    cbh, cfh = _host_consts()
    f = lambda a: np.ascontiguousarray(np.asarray(a, dtype=np.float32))
    shared = {
        "w_ada": f(inputs["w_ada"]), "b_ada": f(inputs["b_ada"]), "norm1_g": f(inputs["norm1_g"]),
        "w_in": f(inputs["w_in"]), "w_br_a": f(inputs["w_br_a"]), "w_br_b": f(inputs["w_br_b"]),
        "w_br_c": f(inputs["w_br_c"]), "w_gate": f(inputs["w_gate"]), "b_gate": f(inputs["b_gate"]),
        "w_out": f(inputs["w_out"]), "norm2_g": f(inputs["norm2_g"]), "w_gu": f(inputs["w_gu"]),
        "w_down": f(inputs["w_down"]), "final_g": f(inputs["final_g"]).reshape(1, D),
        "cb": cbh, "cf": cfh,
    }
    x = f(inputs["x"])
    c = f(inputs["c"])
    in_maps = []
    for b in range(8):
        m = dict(shared)
        m["xT"] = np.ascontiguousarray(x[b].T)
        m["c"] = np.ascontiguousarray(c[b].reshape(8, 128).T)
        in_maps.append(m)
    return in_maps


def kernel(**inputs):
    if "nc" not in _CACHE:
        _CACHE["nc"] = build()[0]
    nc = _CACHE["nc"]
    in_maps = _prep_inputs(inputs)
    res = run_bass_kernel_spmd(nc, in_maps, core_ids=list(range(8)))
    out = np.stack([np.ascontiguousarray(res.results[b]["outT"].T) for b in range(8)], axis=0)
    return out.astype(np.float32)
```

```python
import numpy as np
import concourse.bass as bass
import concourse.mybir as mybir
from concourse.bass_utils import run_bass_kernel_spmd

F32 = mybir.dt.float32
BF16 = mybir.dt.bfloat16
AF = mybir.ActivationFunctionType
ALU = mybir.AluOpType
AX = mybir.AxisListType

COMPUTE = ("pe", "act", "dve", "pool")
EPOCH = 30000
DMA_POOL = 10

D = 1024
SEQ = 2048
DEPTH = 4
HD = 64
NH = 24
MIXW = NH * HD
DFF = 2816
NFC = DFF // 128
EPS = 1e-6
DILS = (1, 4, 16)
BIG = 30000.0
FILLER = False


class Sched:
    def __init__(self, nc, same_engine_sync=True):
        self.nc = nc
        self.same_engine_sync = same_engine_sync
        self.handles = {"pe": nc.tensor, "act": nc.scalar, "dve": nc.vector,
                        "pool": nc.gpsimd, "sp": nc.sync}
        self.q = {e: [] for e in self.handles}
        self.sems = {e: [] for e in COMPUTE}
        self.cnt = {e: 0 for e in COMPUTE}
        self.pending = {e: ([], []) for e in COMPUTE}
        self.unsig = {e: set() for e in COMPUTE}
        self.dma_sems = {}
        self.dma_cnt = {}
        self.res = {}
        self.waited = {e: {} for e in self.handles}
        self.n_instr = 0
        self.defer = None

    def run_deferred(self, tasks, k=None):
        n = len(tasks) if k is None else min(k, len(tasks))
        saved, self.defer = self.defer, None
        for _ in range(n):
            kind, args = tasks.pop(0)
            (self.issue if kind == "i" else self.dma)(*args)
        self.defer = saved

    def _eng_event(self, eng):
        n = self.cnt[eng]
        ep = (n - 1) // EPOCH
        while len(self.sems[eng]) <= ep:
            self.sems[eng].append(self.nc.alloc_semaphore(f"s_{eng}_{len(self.sems[eng])}"))
        return (self.sems[eng][ep], n - ep * EPOCH, eng)

    def _need(self, eng, ev, waits, force=False):
        if ev is None:
            return
        sem, val, src = ev
        if src == eng and eng in COMPUTE and not force:
            if eng == "pe" or not self.same_engine_sync:
                return
        w = self.waited[eng]
        k = id(sem)
        if w.get(k, (None, 0))[1] >= val:
            return
        w[k] = (sem, val)
        waits[k] = (sem, max(val, waits.get(k, (sem, 0))[1]))

    def _check_unsig(self, eng, key):
        for e in COMPUTE:
            if e != eng and key in self.unsig[e]:
                raise RuntimeError(f"resource {key} has unsignaled access on {e}, needed by {eng}")

    def _deps(self, eng, reads, writes, force):
        waits = {}
        for key in reads:
            self._check_unsig(eng, key)
            st = self.res.get(key)
            if st is not None:
                self._need(eng, st[0], waits, force)
        for key in writes:
            self._check_unsig(eng, key)
            st = self.res.get(key)
            if st is not None:
                for ev in [st[0]] + st[1]:
                    if ev is not None and ev[2] == eng and eng in COMPUTE and not force:
                        continue
                    self._need(eng, ev, waits, force)
        return waits

    def issue(self, eng, fn, reads=(), writes=(), signal=True):
        if self.defer is not None:
            self.defer.append(("i", (eng, fn, list(reads), list(writes), signal)))
            return
        waits = self._deps(eng, reads, writes, False)
        inc = None
        pr, pw = self.pending[eng]
        pr.extend(reads)
        pw.extend(writes)
        if signal:
            self.cnt[eng] += 1
            ev = self._eng_event(eng)
            inc = (ev[0], 1)
            self._commit(pr, pw, ev)
            self.pending[eng] = ([], [])
            self.unsig[eng] = set()
        else:
            self.unsig[eng].update(reads)
            self.unsig[eng].update(writes)
        self.q[eng].append((fn, list(waits.values()), inc))
        self.n_instr += 1

    def _commit(self, reads, writes, ev):
        for key in reads:
            st = self.res.setdefault(key, [None, []])
            st[1].append(ev)
        for key in writes:
            self.res[key] = [ev, []]

    def dma(self, queue, fn, reads=(), writes=()):
        if self.defer is not None:
            self.defer.append(("d", (queue, fn, list(reads), list(writes))))
            return
        eng = queue
        if queue not in self.dma_sems:
            self.dma_sems[queue] = [self.nc.alloc_semaphore(f"d_{queue}_{i}") for i in range(DMA_POOL)]
            self.dma_cnt[queue] = 0
        waits = self._deps(eng, reads, writes, True)
        i = self.dma_cnt[queue]
        self.dma_cnt[queue] += 1
        sem = self.dma_sems[queue][i % DMA_POOL]
        rnd = i // DMA_POOL
        if rnd > 0:
            self._need(eng, (sem, 16 * rnd, "dma_" + queue), waits)
        ev = (sem, 16 * (rnd + 1), "dma_" + queue)
        self._commit(list(reads), list(writes), ev)
        self.q[eng].append((fn, list(waits.values()), (sem, 16)))
        self.n_instr += 1

    def barrier(self):
        evs = []
        for e in COMPUTE:
            assert not self.pending[e][0] and not self.pending[e][1], f"unsignaled tail on {e}"
            if self.cnt[e] > 0:
                evs.append(self._eng_event(e))
        for qn, sems in self.dma_sems.items():
            n = self.dma_cnt[qn]
            for slot, sem in enumerate(sems):
                uses = (n - slot + DMA_POOL - 1) // DMA_POOL if n > slot else 0
                if uses > 0:
                    evs.append((sem, 16 * uses, "dma_" + qn))
        for eng in self.handles:
            waits = {}
            for ev in evs:
                self._need(eng, ev, waits, False if ev[2] == eng else True)
            if waits:
                self.q[eng].append((None, list(waits.values()), None))

    def emit(self):
        nc = self.nc
        for e in COMPUTE:
            assert not self.pending[e][0] and not self.pending[e][1], f"unsignaled tail on {e}"
        with nc.Block() as block:
            def mk(eng):
                def body(h):
                    for fn, waits, inc in self.q[eng]:
                        for sem, val in waits:
                            h.wait_ge(sem, val)
                        if fn is None:
                            continue
                        ins = fn(h)
                        if inc is not None:
                            ins.then_inc(inc[0], inc[1])
                return body
            block.tensor(mk("pe"))
            block.scalar(mk("act"))
            block.vector(mk("dve"))
            block.gpsimd(mk("pool"))
            block.sync(mk("sp"))


CB_IDENT = 0
CB_ONES = 128
CB_TRIU_INCL = 256
CB_TRIU_STRICT = 384
CB_BAND2 = 512
CB_NEGU = 768
CB_NEGONES = 896
CB_ROPEC = 1024
CB_ROPES = 1024 + SEQ
CB_E8 = 1024 + 2 * SEQ
CB_W = 1024 + 3 * SEQ
CF_PB = 0
CF_PASTM = 128
CF_OWN = 256
CF_ONES = 384
CF_EPS = 392
CF_M0 = 393
CF_M1 = 394
CF_SW = 400
CF_IDENT = 528
CF_W = 656


def _host_consts():
    cb = np.zeros((128, CB_W), np.float32)
    s = np.arange(128)[:, None]
    t = np.arange(128)[None, :]
    cb[:, CB_IDENT:CB_IDENT + 128] = (s == t)
    cb[:, CB_ONES:CB_ONES + 128] = 1.0
    cb[:, CB_TRIU_INCL:CB_TRIU_INCL + 128] = np.where(s <= t, 0.0, -BIG)
    cb[:, CB_TRIU_STRICT:CB_TRIU_STRICT + 128] = np.where(s < t, 0.0, -BIG)
    t2 = np.arange(256)[None, :]
    cb[:, CB_BAND2:CB_BAND2 + 256] = np.where(((t2 - s) >= 0) & ((t2 - s) <= 128), 0.0, -BIG)
    cb[:, CB_NEGU:CB_NEGU + 128] = -1.0 * (s >= t)
    cb[:, CB_NEGONES:CB_NEGONES + 128] = -1.0
    pos = np.arange(SEQ, dtype=np.float32)
    inv = (np.float32(10000.0) ** (-np.arange(0, HD, 2, dtype=np.float32) / np.float32(HD))).astype(np.float32)
    ang = (pos[None, :] * inv[:, None]).astype(np.float32)
    cos = np.cos(ang).astype(np.float32)
    sin = np.sin(ang).astype(np.float32)
    for p in range(128):
        d = p % 64
        cb[p, CB_ROPEC:CB_ROPEC + SEQ] = cos[d % 32]
        cb[p, CB_ROPES:CB_ROPES + SEQ] = (-sin[d % 32]) if d < 32 else sin[d % 32]
    for n in range(8):
        cb[64 + n, CB_E8 + n * 256: CB_E8 + (n + 1) * 256] = 1.0
    cf = np.zeros((128, CF_W), np.float32)
    for tt in range(16):
        own = tt // 2
        for n in range(8):
            cf[:, CF_PB + tt * 8 + n] = 0.0 if n < own else -1e30
            cf[:, CF_PASTM + tt * 8 + n] = 1.0 if n < own else 0.0
            cf[:, CF_OWN + tt * 8 + n] = 1.0 if n == own else 0.0
    cf[:, CF_ONES:CF_ONES + 8] = 1.0
    cf[:, CF_EPS] = EPS
    cf[:64, CF_M0] = 1.0
    cf[64:, CF_M1] = 1.0
    for k in range(128):
        cf[k, CF_SW + (k + 64) % 128] = 1.0
        cf[k, CF_IDENT + k] = 1.0
    return cb, cf


SB_BASE = 16512
SB_TOP = 229344


def build(n_layers=DEPTH, stop_after=None, dbg=False):
    nc = bass.Bass("TRN2", target_bir_lowering=False)
    ext_in = lambda name, shape: nc.dram_tensor(name, list(shape), F32, kind="ExternalInput").ap()
    xT_d = ext_in("xT", (D, SEQ))
    c_d = ext_in("c", (128, 8))
    w_ada = ext_in("w_ada", (DEPTH, D, 6 * D))
    b_ada = ext_in("b_ada", (DEPTH, 6 * D))
    norm1_g = ext_in("norm1_g", (DEPTH, D))
    w_in = ext_in("w_in", (DEPTH, D, 3 * MIXW))
    w_br_a = ext_in("w_br_a", (DEPTH, 256, D))
    w_br_b = ext_in("w_br_b", (DEPTH, 384, D))
    w_br_c = ext_in("w_br_c", (DEPTH, 384, D))
    w_gate = ext_in("w_gate", (DEPTH, D, 3 * D))
    b_gate = ext_in("b_gate", (DEPTH, 3 * D))
    w_out = ext_in("w_out", (DEPTH, D, D))
    norm2_g = ext_in("norm2_g", (DEPTH, D))
    w_gu = ext_in("w_gu", (DEPTH, D, 2 * DFF))
    w_down = ext_in("w_down", (DEPTH, DFF, D))
    final_g = ext_in("final_g", (1, D))
    cb_d = ext_in("cb", (128, CB_W))
    cf_d = ext_in("cf", (128, CF_W))
    outT_d = nc.dram_tensor("outT", [D, SEQ], F32, kind="ExternalOutput").ap()
    skind = "ExternalOutput" if dbg else "Internal"
    qT_s = nc.dram_tensor("qT_s", [MIXW, SEQ], BF16, kind=skind).ap()
    kT_s = nc.dram_tensor("kT_s", [MIXW, SEQ], BF16, kind=skind).ap()
    v_s = nc.dram_tensor("v_s", [SEQ, MIXW], BF16, kind=skind).ap()
    g_s = nc.dram_tensor("g_s", [3 * D, SEQ], BF16, kind=skind).ap()
    if dbg:
        hT_dbg = nc.dram_tensor("hT_dbg", [D, SEQ], BF16, kind="ExternalOutput").ap()
        oT_dbg = nc.dram_tensor("oT_dbg", [D, SEQ], BF16, kind="ExternalOutput").ap()
        x1_dbg = nc.dram_tensor("x1_dbg", [D, SEQ], F32, kind="ExternalOutput").ap()
        mod_dbg = nc.dram_tensor("mod_dbg", [128, 48], F32, kind="ExternalOutput").ap()

    S = Sched(nc)

    off = [SB_BASE]

    def palloc(name, shape, dt, nbytes):
        t = nc.alloc_sbuf_tensor_at(name, list(shape), dt, offset=off[0])
        off[0] += (nbytes + 63) // 64 * 64
        return t

    xT = palloc("xT", [128, 8, SEQ], F32, 8 * SEQ * 4)
    cb = palloc("cbt", [128, CB_E8], BF16, CB_E8 * 2)
    cf = palloc("cft", [128, CF_W], F32, CF_W * 4)
    modT = palloc("modT", [128, DEPTH, 48], F32, DEPTH * 48 * 4)
    gT = palloc("gT", [128, DEPTH, 16], F32, DEPTH * 16 * 4)
    bgT = palloc("bgT", [128, DEPTH, 24], F32, DEPTH * 24 * 4)
    fgT = palloc("fgT", [128, 8], F32, 32)
    AB = palloc("AB", [128, DEPTH, 16], F32, DEPTH * 16 * 4)
    cin = palloc("cin", [128, 8], F32, 32)
    cact = palloc("cact", [128, 8], F32, 32)
    ARENA = off[0]
    assert ARENA < SB_TOP

    acache = {}

    def alloc_at(name, shape, dt, offset):
        key = (name, offset, tuple(shape), str(dt))
        if key not in acache:
            acache[key] = nc.alloc_sbuf_tensor_at(name, list(shape), dt, offset=offset)
        return acache[key]

    def arena():
        o = [ARENA]

        def a(name, shape, dt, nbytes):
            t = alloc_at(name, shape, dt, o[0])
            o[0] += (nbytes + 63) // 64 * 64
            assert o[0] <= SB_TOP, (name, o[0], SB_TOP)
            return t
        a.o = o
        return a

    ps = [nc.alloc_psum_tensor(f"ps{i}", [128, 512], F32) for i in range(8)]
    PK = lambda i: ("ps", i)

    ident = cb[:, CB_IDENT:CB_IDENT + 128]
    ones_bf = cb[:, CB_ONES:CB_ONES + 128]
    triu_incl = cb[:, CB_TRIU_INCL:CB_TRIU_INCL + 128]
    triu_strict = cb[:, CB_TRIU_STRICT:CB_TRIU_STRICT + 128]
    band2 = cb[:, CB_BAND2:CB_BAND2 + 256]
    negU = cb[:, CB_NEGU:CB_NEGU + 128]
    negones = cb[:, CB_NEGONES:CB_NEGONES + 128]
    ropeC = cb[:, CB_ROPEC:CB_ROPEC + SEQ]
    ropeS = cb[:, CB_ROPES:CB_ROPES + SEQ]
    ones_f = cf[:, CF_ONES:CF_ONES + 8]
    eps_c = cf[:, CF_EPS:CF_EPS + 1]
    half_m = (cf[:, CF_M0:CF_M0 + 1], cf[:, CF_M1:CF_M1 + 1])
    swap_f = cf[:, CF_SW:CF_SW + 128]
    ident_f = cf[:, CF_IDENT:CF_IDENT + 128]

    def mm(out, lhsT, rhs, start, stop, reads, writes, signal, skip=False):
        S.issue("pe", lambda h, o=out, l=lhsT, r=rhs, a=start, b=stop, sk=skip:
                h.matmul(o, lhsT=l, rhs=r, start=a, stop=b, skip_group_check=sk),
                reads, writes, signal)

    def act(out, in_, func, reads, writes, scale=None, bias=None):
        kw = {}
        if scale is not None:
            kw["scale"] = scale
        if bias is not None:
            kw["bias"] = bias
        S.issue("act", lambda h, o=out, i=in_, f=func, kw=kw: h.activation(out=o, in_=i, func=f, **kw),
                reads, writes)

    def tt(eng, out, in0, in1, op, reads, writes):
        S.issue(eng, lambda h, o=out, a=in0, b=in1, p=op: h.tensor_tensor(out=o, in0=a, in1=b, op=p),
                reads, writes)

    def ts(eng, out, in0, s1, op0, reads, writes, s2=None, op1=None):
        if op1 is None:
            S.issue(eng, lambda h, o=out, a=in0, s=s1, p=op0: h.tensor_scalar(out=o, in0=a, scalar1=s, scalar2=None, op0=p),
                    reads, writes)
        else:
            S.issue(eng, lambda h, o=out, a=in0, s=s1, p=op0, s2=s2, p1=op1:
                    h.tensor_scalar(out=o, in0=a, scalar1=s, scalar2=s2, op0=p, op1=p1), reads, writes)

    def stt(out, in0, scalar, in1, op0, op1, reads, writes):
        S.issue("dve", lambda h, o=out, a=in0, s=scalar, b=in1, p0=op0, p1=op1:
                h.scalar_tensor_tensor(out=o, in0=a, scalar=s, in1=b, op0=p0, op1=p1), reads, writes)

    def cp(eng, out, in_, reads, writes):
        if eng == "act":
            S.issue("act", lambda h, o=out, i=in_: h.copy(out=o, in_=i), reads, writes)
        else:
            S.issue(eng, lambda h, o=out, i=in_: h.tensor_copy(out=o, in_=i), reads, writes)

    def dma(queue, out, in_, reads, writes, slow=False):
        S.dma(queue, lambda h, o=out, i=in_, sl=slow: h.dma_start(out=o, in_=i, allow_slow_non_contiguous=sl),
              reads, writes)

    XK = lambda kc, n: ("xT", kc, n)
    XK_ALL = [XK(kc, n) for kc in range(8) for n in range(4)]

    for c0 in range(0, CB_E8, 1024):
        dma("pool", cb[:, c0:c0 + 1024], cb_d[:, c0:c0 + 1024], [], [("cbl", c0)])
    dma("sp", cf[:], cf_d, [], ["cf"])
    dma("sp", cin[:], c_d, [], ["cin"])
    for kc in range(8):
        dma("pool", xT[:, kc, :], xT_d[kc * 128:(kc + 1) * 128, :], [], [XK(kc, n) for n in range(4)])
    act(cact[:], cin[:], AF.Silu, ["cin"], ["cact"])

    A = arena()
    rows_sv = [A(f"rsv_{l}", [40, 128], F32, 512) for l in range(n_layers)]
    rows_fg = A("rfg", [8, 128], F32, 512)
    acc_h0 = A("acch", [128, 3 * D], F32, 3 * D * 4)
    wap0 = [A(f"wap{i}", [128, 512], F32, 2048) for i in range(8)]
    brow0 = A("brow", [48, 128], F32, 512)
    bT0 = A("bT", [128, 48], F32, 192)

    def row_to_cols(row_ap, ncols, bank, dst, dst_key, rkey):
        for j in range(ncols):
            mm(ps[bank][:, j:j + 1], row_ap[0:1, j * 128:(j + 1) * 128], ones_f[0:1, 0:1],
               True, True, [rkey, "cf"], [PK(bank)], j == ncols - 1)
        cp("act", dst, ps[bank][:, 0:ncols], [PK(bank)], [dst_key])

    def emit_adaln(l, acc_h, wap, brow, bT, dep_keys, b1, b2):
        idx = 0
        for half in range(2):
            for kc in range(8):
                for pcs in range(6):
                    c0 = half * 3072 + pcs * 512
                    nw = len(wap)
                    w = wap[idx % nw]
                    dma("sp", w[:], w_ada[l, kc * 128:(kc + 1) * 128, c0:c0 + 512], dep_keys if idx < nw else [],
                        [("wap", idx % nw)])
                    dst = acc_h[:, pcs * 512:(pcs + 1) * 512]
                    if kc == 0:
                        ts("dve", dst, w[:], cact[:, 0:1], ALU.mult, [("wap", idx % nw), "cact"] + list(dep_keys),
                           [("acch", pcs)])
                    else:
                        stt(dst, w[:], cact[:, kc:kc + 1], dst, ALU.mult, ALU.add,
                            [("wap", idx % nw), "cact", ("acch", pcs)], [("acch", pcs)])
                    idx += 1
            for jj in range(24):
                j = half * 24 + jj
                mm(ps[b1][:, j:j + 1], acc_h[:, jj * 128:(jj + 1) * 128], ones_f[:, 0:1], True, True,
                   [("acch", jj // 4), "cf"], [PK(b1)], jj == 23)
        dma("sp", brow[:, :], b_ada[l].rearrange("(a b) -> a b", b=128), list(dep_keys), ["brow"])
        S.issue("pe", lambda h: h.transpose(ps[b2][:, 0:48], brow[:, :], ident_f[0:48, 0:48]), ["brow", "cf"], [PK(b2)])
        cp("act", bT[:, :], ps[b2][:, 0:48], [PK(b2)], ["bT"])
        tt("dve", modT[:, l, :], ps[b1][:, 0:48], bT[:, :], ALU.add, [PK(b1), "bT"], [("modT", l)])
        for w_, sc0 in ((0, 8), (1, 32)):
            ts("dve", AB[:, l, w_ * 8:(w_ + 1) * 8], modT[:, l, sc0:sc0 + 8], 1.0, ALU.add,
               [("modT", l)], [("AB", l, w_)])
            tt("dve", AB[:, l, w_ * 8:(w_ + 1) * 8], AB[:, l, w_ * 8:(w_ + 1) * 8], gT[:, l, w_ * 8:(w_ + 1) * 8],
               ALU.mult, [("AB", l, w_), ("gT", l, w_)], [("AB", l, w_)])

    def small_vectors(l):
        t = rows_sv[l]
        dma("pool", t[0:8, :], norm1_g[l].rearrange("(a b) -> a b", b=128), [], [("rsv", l, 0)])
        dma("pool", t[8:16, :], norm2_g[l].rearrange("(a b) -> a b", b=128), [], [("rsv", l, 1)])
        dma("pool", t[16:40, :], b_gate[l].rearrange("(a b) -> a b", b=128), [], [("rsv", l, 2)])
        S.issue("pe", lambda h, t=t: h.transpose(ps[1][:, 0:40], t[:, :], ident_f[0:40, 0:40]),
                [("rsv", l, 0), ("rsv", l, 1), ("rsv", l, 2), "cf"], [PK(1)])
        cp("act", gT[:, l, 0:16], ps[1][:, 0:16], [PK(1)], [("gT", l, 0), ("gT", l, 1)])
        cp("act", bgT[:, l, :], ps[1][:, 16:40], [PK(1)], [("bgT", l)])
    small_vectors(0)
    emit_adaln(0, acc_h0, wap0, brow0, bT0, [], 2, 3)
    for l in range(1, n_layers):
        small_vectors(l)
    dma("pool", rows_fg[:, :], final_g[0].rearrange("(a b) -> a b", b=128), [], ["rfg"])
    S.issue("pe", lambda h: h.transpose(ps[1][:, 0:8], rows_fg[:, :], ident_f[0:8, 0:8]), ["rfg", "cf"], [PK(1)])
    cp("act", fgT[:, :], ps[1][:, 0:8], [PK(1)], ["fgT"])
    if dbg:
        dma("sp", mod_dbg, modT[:, 0, :], [("modT", 0)], ["mod_dbg"])
    S.barrier()

    def norm_phase(hT, scale_ap, bias_ap, skeys, A, tag, f32_out=None):
        sq = A(f"sq{tag}", [128, 8, 512], BF16, 8 * 512 * 2)
        rs = [A(f"rs{tag}{i}", [128, 512], F32, 2048) for i in range(2)]
        tmp = [A(f"tmp{tag}{i}", [128, 512], F32, 2048) for i in range(2)]
        BANK = 7
        for n in range(4):
            cols = slice(n * 512, (n + 1) * 512)
            for kc in range(8):
                act(sq[:, kc, :], xT[:, kc, cols], AF.Square, [XK(kc, n)], [("sq", kc)])
                mm(ps[BANK][:, :], ones_bf, sq[:, kc, :], kc == 0, kc == 7, [("sq", kc), "cb"], [PK(BANK)], kc == 7)
            r = rs[n % 2]
            act(r[:], ps[BANK][:, :], AF.Sqrt, [PK(BANK), "cf"], [("rs", n % 2)], scale=1.0 / D, bias=eps_c)
            S.issue("dve", lambda h, o=r: h.reciprocal(out=o[:], in_=o[:]), [("rs", n % 2)], [("rs", n % 2)])
            for kc in range(8):
                t_ = tmp[kc % 2]
                tt("dve", t_[:], xT[:, kc, cols], r[:], ALU.mult, [XK(kc, n), ("rs", n % 2)], [("tmp", kc % 2)])
                if f32_out is None:
                    if bias_ap is not None:
                        act(hT[:, kc, cols], t_[:], AF.Identity, [("tmp", kc % 2)] + skeys, [("hT", kc, n)],
                            scale=scale_ap[:, kc:kc + 1], bias=bias_ap[:, kc:kc + 1])
                    else:
                        act(hT[:, kc, cols], t_[:], AF.Identity, [("tmp", kc % 2)] + skeys, [("hT", kc, n)],
                            scale=scale_ap[:, kc:kc + 1])
                else:
                    act(xT[:, kc, cols], t_[:], AF.Identity, [("tmp", kc % 2)] + skeys, [XK(kc, n)],
                        scale=scale_ap[:, kc:kc + 1])
                    dma("sp", outT_d[kc * 128:(kc + 1) * 128, cols], xT[:, kc, cols], [XK(kc, n)], [("out", kc, n)])

    def rope(dst, qa, qb, npart, A_tmp, keys_in, key_out):
        t1, t2 = A_tmp
        pr = slice(0, npart)

        def add_(n):
            cols = slice(n * 512, (n + 1) * 512)
            tt("dve", dst[pr, cols], t1[n % 2][pr, :], t2[n % 2][pr, :], ALU.add,
               [("ropet1", n % 2), ("ropet2", n % 2)], [(key_out, n)])
        for n in range(4):
            cols = slice(n * 512, (n + 1) * 512)
            tt("dve", t1[n % 2][pr, :], qa[pr, cols], ropeC[pr, cols], ALU.mult,
               list(keys_in[0]) + ["cb"], [("ropet1", n % 2)])
            tt("dve", t2[n % 2][pr, :], qb[pr, cols], ropeS[pr, cols], ALU.mult,
               list(keys_in[1]) + ["cb"], [("ropet2", n % 2)])
            if n > 0:
                add_(n - 1)
        add_(3)

    def load_swapped(dst, src_rows, nhalf, key):
        keys = []
        for hh in range(nhalf):
            b = hh * 64
            dma("sp", dst[b:b + 32, :], src_rows[b + 32:b + 64, :], [], [(key, 2 * hh)])
            dma("sp", dst[b + 32:b + 64, :], src_rows[b:b + 32, :], [], [(key, 2 * hh + 1)])
            keys += [(key, 2 * hh), (key, 2 * hh + 1)]
        return keys

    for l in range(n_layers):
        A = arena()
        hT = A("hT", [128, 8, SEQ], BF16, 8 * SEQ * 2)
        wb = [A(f"wb{i}", [128, 8, 512], BF16, 8 * 512 * 2) for i in range(3)]
        stq = [A(f"stq{i}", [128, SEQ], BF16, SEQ * 2) for i in range(2)]
        stv = [A(f"stv{i}", [128, 512], BF16, 1024) for i in range(2)]
        norm_phase(hT, AB[:, l, 0:8], modT[:, l, 0:8], [("AB", l, 0), ("modT", l)], A, "a")
        HK_ALL = [("hT", kc, n) for kc in range(8) for n in range(4)]
        if dbg and l == 0:
            for kc in range(8):
                dma("sp", hT_dbg[kc * 128:(kc + 1) * 128, :], hT[:, kc, :], [("hT", kc, n) for n in range(4)], ["hT_dbg"])
        w_in_v = w_in[l].rearrange("(kc p) c -> p kc c", p=128)
        w_gate_v = w_gate[l].rearrange("(kc p) c -> p kc c", p=128)
        groups = [("in", g) for g in range(9)] + [("gate", g) for g in range(6)]
        bank_i = [0]
        ev_i = [0]

        def load_group(gi):
            kind, g = groups[gi]
            src = (w_in_v if kind == "in" else w_gate_v)[:, :, g * 512:(g + 1) * 512]
            dma("pool", wb[gi % 3][:], src, [], [("wb", gi % 3)])
        load_group(0)
        load_group(1)
        for gi, (kind, g) in enumerate(groups):
            if gi + 2 < len(groups):
                load_group(gi + 2)
            wt = wb[gi % 3]
            wk = ("wb", gi % 3)
            if kind == "in" and g >= 6:
                for tt_ in range(16):
                    b = bank_i[0] % 4
                    bank_i[0] += 1
                    for kc in range(8):
                        mm(ps[b][:, :], hT[:, kc, tt_ * 128:(tt_ + 1) * 128], wt[:, kc, :], kc == 0, kc == 7,
                           [("hT", kc, tt_ // 4), wk], [PK(b)], kc == 7)
                    sv = stv[tt_ % 2]
                    if ev_i[0] % 2 == 0:
                        cp("act", sv[:], ps[b][:, :], [PK(b)], [("stv", tt_ % 2)])
                    else:
                        cp("dve", sv[:], ps[b][:, :], [PK(b)], [("stv", tt_ % 2)])
                    ev_i[0] += 1
                    dma("sp", v_s[tt_ * 128:(tt_ + 1) * 128, (g - 6) * 512:(g - 5) * 512], sv[:],
                        [("stv", tt_ % 2)], [("v_s", g - 6, tt_)])
            else:
                for cc in range(4):
                    fc = g * 4 + cc
                    st = stq[fc % 2]
                    sk = ("stq", fc % 2)
                    for n in range(4):
                        b = bank_i[0] % 4
                        bank_i[0] += 1
                        cols = slice(n * 512, (n + 1) * 512)
                        for kc in range(8):
                            mm(ps[b][:, :], wt[:, kc, cc * 128:(cc + 1) * 128], hT[:, kc, cols], kc == 0, kc == 7,
                               [("hT", kc, n), wk], [PK(b)], kc == 7)
                        if kind == "gate":
                            act(st[:, cols], ps[b][:, :], AF.Sigmoid, [PK(b), ("bgT", l)], [sk],
                                bias=bgT[:, l, fc:fc + 1])
                        else:
                            sc = 0.125 if fc < 12 else 1.0
                            if ev_i[0] % 2 == 0:
                                act(st[:, cols], ps[b][:, :], AF.Copy, [PK(b)], [sk], scale=sc)
                            else:
                                ts("dve", st[:, cols], ps[b][:, :], sc, ALU.mult, [PK(b)], [sk])
                            ev_i[0] += 1
                    if kind == "gate":
                        dma("sp", g_s[fc * 128:(fc + 1) * 128, :], st[:], [sk], [("g_s", fc)])
                    elif fc < 12:
                        dma("sp", qT_s[fc * 128:(fc + 1) * 128, :], st[:], [sk], [("qT_s", fc)])
                    else:
                        dma("sp", kT_s[(fc - 12) * 128:(fc - 11) * 128, :], st[:], [sk], [("kT_s", fc - 12)])
        S.barrier()
        if stop_after == "bc":
            break

        v_nat = v_s.rearrange("(kt p) c -> p kt c", p=128)
        tasks = []

        def record(fn, *a):
            S.defer = tasks
            fn(*a)
            S.defer = None

        def pipeline(blocks, stages, drip=2):
            n = len(blocks)
            maxs = max(sk for _, sk in stages)
            for it in range(n + maxs):
                S.run_deferred(tasks, drip)
                for fn, sk in stages:
                    i = it - sk
                    if 0 <= i < n:
                        fn(blocks[i], i)
            S.run_deferred(tasks)

        def dump_oT():
            for kc in range(8):
                dma("sp", oT_dbg[kc * 128:(kc + 1) * 128, :], oT[:, kc, :], [], ["oT_dbg"])

        A = arena()
        oT = A("oT", [128, 8, SEQ], BF16, 8 * SEQ * 2)
        vtm = A("vtm", [128, 16, 3, 192], BF16, 16 * 3 * 192 * 2)
        M_Pb = [A(f"Pbm{i}", [128, 512], BF16, 1024) for i in range(4)]
        M_Rt = [A(f"Rtm{i}", [128, 512], F32, 2048) for i in range(2)]
        M_Rs = [A(f"Rsm{i}", [128, 512], F32, 2048) for i in range(2)]
        SB_ARENA = 83968
        assert A.o[0] <= ARENA + SB_ARENA
        A.o[0] = ARENA + SB_ARENA
        M_qa = A("qa", [128, SEQ], BF16, SEQ * 2)
        M_qb = A("qb", [128, SEQ], BF16, SEQ * 2)
        M_ka = A("ka", [128, SEQ], BF16, SEQ * 2)
        M_kb = A("kb", [128, SEQ], BF16, SEQ * 2)
        M_qrs = [A(f"qg{i}", [128, SEQ], BF16, SEQ * 2) for i in range(2)]
        M_krs = [A(f"kg{i}", [128, SEQ], BF16, SEQ * 2) for i in range(2)]
        M_rt1 = [A(f"rt1m{i}", [128, 512], BF16, 1024) for i in range(2)]
        M_rt2 = [A(f"rt2m{i}", [128, 512], BF16, 1024) for i in range(2)]
        M_btin = A("btin", [128, 16, 72], BF16, 16 * 72 * 2)
        M_cmpb = A("cmpb", [128, 16, 8, 8], F32, 4096)
        M_gm = A("gm", [128, 16, 8], F32, 512)
        M_rank = A("rank", [128, 16, 8], F32, 512)
        M_kmf = A("kmf", [64, 8], F32, 32)
        M_kmb = A("kmb", [64, 8], BF16, 16)
        for i_ in range(2):
            for c0 in range(0, SEQ, 1024):
                dma("pool", M_krs[i_][64:72, c0:c0 + 1024], cb_d[64:72, CB_E8 + c0:CB_E8 + c0 + 1024], [], [("kaug_e", i_, c0)])
        S.issue("pool", lambda h: h.memset(M_btin[:], 0.0), [], ["btin"])
        PBt = cf[:, CF_PB:CF_PB + 128].rearrange("p (a b) -> p a b", b=8)
        PASTMt = cf[:, CF_PASTM:CF_PASTM + 128].rearrange("p (a b) -> p a b", b=8)
        OWNt = cf[:, CF_OWN:CF_OWN + 128].rearrange("p (a b) -> p a b", b=8)

        def moba_prep(hm):
            sl = hm % 2
            qaug, kaug = M_qrs[sl], M_krs[sl]
            fr = 768 + hm * 64
            dma("sp", M_qa[0:64, :], qT_s[fr:fr + 64, :], [], ["qa"])
            qbk = load_swapped(M_qb, qT_s[fr:fr + 64, :], 1, "qb")
            dma("sp", M_ka[0:64, :], kT_s[fr:fr + 64, :], [], ["ka"])
            kbk = load_swapped(M_kb, kT_s[fr:fr + 64, :], 1, "kb")
            rope(qaug, M_qa, M_qb, 64, (M_rt1, M_rt2), [["qa"], qbk], ("qaug", sl))
            rope(kaug, M_ka, M_kb, 64, (M_rt1, M_rt2), [["ka"], kbk], ("kaug", sl))
            QK_ = [(("qaug", sl), n) for n in range(4)]
            KK_ = [(("kaug", sl), n) for n in range(4)]
            S.issue("dve", lambda h, ka_=kaug: h.tensor_reduce(out=M_kmf[:, :], in_=ka_[0:64, :].rearrange("p (n s) -> p n s", s=256),
                                                                axis=AX.X, op=ALU.add), KK_, ["kmf"])
            ts("dve", M_kmb[:, :], M_kmf[:, :], 1.0 / 256.0, ALU.mult, ["kmf"], ["kmb"])
            for t_ in range(16):
                mm(ps[6][:, t_ * 8:(t_ + 1) * 8], qaug[0:64, t_ * 128:(t_ + 1) * 128], M_kmb[:, :], True, True,
                   QK_ + ["kmb"], [PK(6)], t_ == 15)
            gps = ps[6][:, 0:128].rearrange("p (a b) -> p a b", b=8)
            tt("dve", M_gm[:], gps, PBt, ALU.add, [PK(6), "cf"], ["gm"])
            tt("dve", M_cmpb[:], M_gm[:].unsqueeze(2).to_broadcast([128, 16, 8, 8]),
               M_gm[:].unsqueeze(3).to_broadcast([128, 16, 8, 8]), ALU.is_gt, ["gm"], ["cmpb"])
            S.issue("dve", lambda h: h.tensor_reduce(out=M_rank[:], in_=M_cmpb[:], axis=AX.X, op=ALU.add), ["cmpb"], ["rank"])
            ts("dve", M_rank[:], M_rank[:], 2.5, ALU.is_lt, ["rank"], ["rank"])
            tt("dve", M_rank[:], M_rank[:], PASTMt, ALU.mult, ["rank", "cf"], ["rank"])
            tt("dve", M_rank[:], M_rank[:], OWNt, ALU.add, ["rank", "cf"], ["rank"])
            ts("dve", M_btin[:, :, 64:72], M_rank[:], -1.0, ALU.add, ["rank"], ["btin"], s2=BIG, op1=ALU.mult)
            tps = ps[7][:, :].bitcast(BF16)
            for half in range(2):
                for t_ in range(8):
                    tq = half * 8 + t_
                    S.issue("pe", lambda h, o=tps[0:72, t_ * 128:(t_ + 1) * 128], i=M_btin[:, tq, :]:
                            h.transpose(o, i, ident), ["btin", "cb"], [PK(7)], t_ == 7)
                cp("act", qaug[64:72, half * 1024:(half + 1) * 1024], tps[64:72, 0:1024], [PK(7)], [("qaug_b", sl, half)])
        M_blocks = []
        M_gi = 0
        for hm in range(6):
            for qc in range(4):
                last_kt = 4 * qc + 3
                for kt in range(last_kt, -1, -1):
                    diag = kt >= 4 * qc
                    o_ = (kt - 4 * qc) * 128 if diag else 0
                    M_blocks.append(dict(hm=hm, qc=qc, kt=kt, diag=diag, o=o_, first=kt == last_kt, last=kt == 0, g=M_gi,
                                       newhead=(qc == 0 and kt == last_kt)))
                M_gi += 1

        def mb_s1(b, i):
            hm = b["hm"]
            if b["newhead"]:
                S.run_deferred(tasks)
                if hm + 1 < 6:
                    record(moba_prep, hm + 1)
            sl = hm % 2
            o_ = b["o"]
            qcols = slice(b["qc"] * 512 + o_, (b["qc"] + 1) * 512)
            kcols = slice(b["kt"] * 128, (b["kt"] + 1) * 128)
            deps = [(("qaug", sl), n) for n in range(4)] + [(("kaug", sl), n) for n in range(4)] + \
                   [("qaug_b", sl, 0), ("qaug_b", sl, 1), ("kaug_e", sl, 0), ("kaug_e", sl, 1024)]
            if b["diag"]:
                mm(ps[i % 2][:, o_:o_ + 128], ident, triu_incl, True, False, ["cb"], [PK(i % 2)], False, skip=True)
            mm(ps[i % 2][:, o_:512], M_krs[sl][0:72, kcols], M_qrs[sl][0:72, qcols], not b["diag"], True, deps, [PK(i % 2)],
               True, skip=True)

        def mb_s2(b, i):
            o_ = b["o"]
            act(M_Pb[i % 4][:, o_:512], ps[i % 2][:, o_:512], AF.Exp, [PK(i % 2)], [("Pb", i % 4)])

        def mb_s3(b, i):
            hm = b["hm"]
            pc, hf = divmod(hm, 2)
            pr = slice(hf * 64, hf * 64 + 64)
            po = slice((1 - hf) * 64, (1 - hf) * 64 + 64)
            o_ = b["o"]
            bn = 2 + b["g"] % 2
            mm(ps[bn][:, o_:512], vtm[:, b["kt"], pc, hf * 64:hf * 64 + 128], M_Pb[i % 4][:, o_:512], b["first"], b["last"],
               [("vtm", pc, 0), ("vtm", pc, 1), "vtm1", ("Pb", i % 4)], [PK(bn)], True, skip=True)

        def mb_s4(b, i):
            if not b["last"]:
                return
            hf = b["hm"] % 2
            po = slice((1 - hf) * 64, (1 - hf) * 64 + 64)
            bn = 2 + b["g"] % 2
            g2 = b["g"] % 2
            R = M_Rt[g2]
            act(R[po, :], ps[bn][po, :], AF.Ln, [PK(bn)], [("Rt", g2)])

        def mb_s4b(b, i):
            if not b["last"]:
                return
            hf = b["hm"] % 2
            po = slice((1 - hf) * 64, (1 - hf) * 64 + 64)
            g2 = b["g"] % 2
            R = M_Rt[g2]
            act(R[po, :], R[po, :], AF.Exp, [("Rt", g2)], [("Rt", g2)], scale=-1.0)

        def mb_s5(b, i):
            if not b["last"]:
                return
            g2 = b["g"] % 2
            mm(ps[4 + g2][:, :], swap_f, M_Rt[g2][:, :], True, True, [("Rt", g2), "cf"], [PK(4 + g2)], True)

        def mb_s6(b, i):
            if not b["last"]:
                return
            pc, hf = divmod(b["hm"], 2)
            pr = slice(hf * 64, hf * 64 + 64)
            bn = 2 + b["g"] % 2
            g2 = b["g"] % 2
            cols = slice(b["qc"] * 512, (b["qc"] + 1) * 512)
            cp("act", M_Rs[g2][pr, :], ps[4 + g2][pr, :], [PK(4 + g2)], [("Rs", g2)])
            tt("dve", oT[pr, 2 + pc, cols], ps[bn][pr, :], M_Rs[g2][pr, :], ALU.mult, [PK(bn), ("Rs", g2)],
               [("oT", 2 + pc, b["qc"])])

        A = arena()
        oT = A("oT", [128, 8, SEQ], BF16, 8 * SEQ * 2)
        vt = A("vt", [128, 16, 384], BF16, 16 * 384 * 2)
        QS = [A(f"sbq{i}", [128, SEQ], BF16, SEQ * 2) for i in range(2)]
        KZ = [[A(f"sbk{i}{h}", [128, SEQ], BF16, SEQ * 2) for h in range(2)] for i in range(2)]
        Pb = [A(f"Pb{i}", [128, 512], BF16, 1024) for i in range(4)]
        Ef = [A(f"Ef{i}", [128, 512], BF16, 1024) for i in range(3)]
        SPb = [A(f"SPb{i}", [128, 512], BF16, 1024) for i in range(4)]
        SPsb = [A(f"SPsb{i}", [128, 512], BF16, 1024) for i in range(2)]
        assert A.o[0] <= ARENA + SB_ARENA, A.o[0] - ARENA
        dma("sp", vt[:, :, :], v_nat[:, :, 1152:1536], [], ["vt"])
        for i_ in range(2):
            for h_ in range(2):
                S.issue("pool", lambda h, t=KZ[i_][h_]: h.memset(t[:], 0.0), [], [("sbk", i_, h_)])

        def sb_load(pc):
            ch = 9 + pc
            sl = pc % 2
            dma("sp", QS[sl][:], qT_s[ch * 128:(ch + 1) * 128, :], [], [("sbq", sl)])
            for h_ in range(2):
                dma("sp", KZ[sl][h_][h_ * 64:(h_ + 1) * 64, :], kT_s[ch * 128 + h_ * 64:ch * 128 + (h_ + 1) * 64, :], [],
                    [("sbk", sl, h_)])
        blocks = []
        gi = 0
        for pc in range(3):
            for hf in range(2):
                for qc in range(4):
                    last_kt = 4 * qc + 3
                    for kt in range(last_kt, -1, -1):
                        diag = kt >= 4 * qc
                        o_ = (kt - 4 * qc) * 128 if diag else 0
                        o_pv = (kt + 1 - 4 * qc) * 128 if (kt + 1) >= 4 * qc else 0
                        blocks.append(dict(pc=pc, hf=hf, qc=qc, kt=kt, diag=diag, o=o_, opv=o_pv, first=kt == last_kt,
                                           last=kt == 0, g=gi, newpair=(hf == 0 and qc == 0 and kt == last_kt)))
                    gi += 1
        sb_load(0)

        def sb_qk(b, bank, stop, sig):
            o_ = b["o"]
            sl = b["pc"] % 2
            qcols = slice(b["qc"] * 512 + o_, (b["qc"] + 1) * 512)
            kcols = slice(b["kt"] * 128, (b["kt"] + 1) * 128)
            if b["diag"]:
                mm(ps[bank][:, o_:o_ + 128], ident, triu_strict, True, False, ["cb"], [PK(bank)], False, skip=True)
            mm(ps[bank][:, o_:512], KZ[sl][b["hf"]][:, kcols], QS[sl][:, qcols], not b["diag"], stop,
               [("sbq", sl), ("sbk", sl, b["hf"])], [PK(bank)], sig, skip=True)

        def sb_s1(b, i):
            if b["newpair"] and b["pc"] + 1 < 3:
                sb_load(b["pc"] + 1)
            sb_qk(b, i % 2, True, True)

        def sb_s2(b, i):
            o_ = b["o"]
            act(Ef[i % 3][:, o_:512], ps[i % 2][:, o_:512], AF.Exp, [PK(i % 2)], [("Ef", i % 3)])

        def sb_s3(b, i):
            o_ = b["o"]
            act(SPb[i % 4][:, o_:512], Ef[i % 3][:, o_:512], AF.Ln, [("Ef", i % 3)], [("SPb", i % 4)], bias=1.0)

        def sb_s45(b, i):
            o_, opv = b["o"], b["opv"]
            ba = 2 + i % 2
            sb_qk(b, ba, False, False)
            if not b["first"]:
                mm(ps[ba][:, opv:512], negones, SPsb[(i - 1) % 2][:, opv:512], False, False,
                   [("SPsb", (i - 1) % 2), "cb"], [PK(ba)], False, skip=True)
            mm(ps[ba][:, o_:512], negU, SPb[i % 4][:, o_:512], False, True, [("SPb", i % 4), "cb"], [PK(ba)],
               True, skip=True)
            if not b["last"]:
                new = SPsb[i % 2]
                if b["first"]:
                    cp("dve", new[:, o_:512], SPb[i % 4][:, o_:512], [("SPb", i % 4)], [("SPsb", i % 2)])
                else:
                    if opv > o_:
                        cp("dve", new[:, o_:opv], SPb[i % 4][:, o_:opv], [("SPb", i % 4)], [("SPsb", i % 2)])
                    tt("dve", new[:, opv:512], SPsb[(i - 1) % 2][:, opv:512], SPb[i % 4][:, opv:512], ALU.add,
                       [("SPsb", (i - 1) % 2), ("SPb", i % 4)], [("SPsb", i % 2)])

        def sb_s6(b, i):
            o_ = b["o"]
            act(Pb[i % 4][:, o_:512], ps[2 + i % 2][:, o_:512], AF.Exp, [PK(2 + i % 2)], [("Pb", i % 4)])

        def sb_s7(b, i):
            pr = slice(b["hf"] * 64, b["hf"] * 64 + 64)
            o_ = b["o"]
            bo = 4 + b["g"] % 2
            mm(ps[bo][:, o_:512], vt[:, b["kt"], b["pc"] * 128:(b["pc"] + 1) * 128], Pb[i % 4][:, o_:512],
               b["first"], b["last"], ["vt", ("Pb", i % 4)], [PK(bo)], True, skip=True)
            if b["last"]:
                cols = slice(b["qc"] * 512, (b["qc"] + 1) * 512)
                cp("dve", oT[pr, 5 + b["pc"], cols], ps[bo][pr, :], [PK(bo)], [("oT", 5 + b["pc"], b["qc"])])
        record(moba_prep, 0)
        pipeline(blocks, [(sb_s1, 0), (sb_s2, 1), (sb_s3, 2), (sb_s45, 3), (sb_s6, 4), (sb_s7, 5)], drip=1)
        S.barrier()
        if stop_after == "sb":
            dump_oT()
            break

        for p_ in range(3):
            for e_ in range(2):
                c0 = 768 + p_ * 128 + e_ * 64
                dma("sp", vtm[:, :, p_, e_ * 128:e_ * 128 + 64], v_nat[:, :, c0:c0 + 64], [], [("vtm", p_, e_)])
        S.issue("pool", lambda h: h.memset(vtm[:, :, :, 64:128], 1.0), [], ["vtm1"])
        for i_ in range(2):
            S.issue("pool", lambda h, t=M_Rt[i_]: h.memset(t[:], 1.0), [], [("Rt", i_)])
        pipeline(M_blocks, [(mb_s1, 0), (mb_s2, 1), (mb_s3, 2), (mb_s4, 3), (mb_s4b, 4), (mb_s5, 5), (mb_s6, 6)], drip=2)
        S.barrier()

        A = arena()
        oT = A("oT", [128, 8, SEQ], BF16, 8 * SEQ * 2)
        vt2 = [A(f"vtd{i}", [128, 16, 192], BF16, 16 * 192 * 2) for i in range(2)]
        qa = A("qa", [128, SEQ], BF16, SEQ * 2)
        qb = A("qb", [128, SEQ], BF16, SEQ * 2)
        ka = A("ka", [128, SEQ], BF16, SEQ * 2)
        kb = A("kb", [128, SEQ], BF16, SEQ * 2)
        qrs = [A(f"qr{i}", [128, SEQ], BF16, SEQ * 2) for i in range(2)]
        krs = [A(f"kr{i}", [128, SEQ], BF16, SEQ * 2) for i in range(2)]
        kz = [[A(f"kz{i}{h}", [128, SEQ], BF16, SEQ * 2) for h in range(2)] for i in range(2)]
        rt1 = [A(f"rt1{i}", [128, 512], BF16, 1024) for i in range(2)]
        rt2 = [A(f"rt2{i}", [128, 512], BF16, 1024) for i in range(2)]
        Pb = [A(f"Pbd{i}", [128, 512], BF16, 1024) for i in range(4)]
        ACC = [A(f"ACC{h}", [128, SEQ], F32, SEQ * 4) for h in range(2)]
        Rt = [A(f"Rt{i}", [128, 512], F32, 2048) for i in range(2)]
        Rs = [A(f"Rs{i}", [128, 512], F32, 2048) for i in range(2)]
        for i_ in range(2):
            S.issue("pool", lambda h, t=vt2[i_]: h.memset(t[:, :, 64:128], 1.0), [], [("vtd1", i_)])

        def dil_prep(sc, g, slot):
            dl = DILS[g]
            nb = SEQ // dl // 128
            ch = 2 * g + sc
            dma("sp", qa[:], qT_s[ch * 128:(ch + 1) * 128, :], [], ["qa"])
            qbk = load_swapped(qb, qT_s[ch * 128:(ch + 1) * 128, :], 2, "qb")
            dma("sp", ka[:], kT_s[ch * 128:(ch + 1) * 128, :], [], ["ka"])
            kbk = load_swapped(kb, kT_s[ch * 128:(ch + 1) * 128, :], 2, "kb")
            vc0 = (4 * g + 2 * sc) * 64
            for r in range(dl):
                for h_ in range(2):
                    src = v_s[r:SEQ:dl, vc0 + h_ * 64:vc0 + (h_ + 1) * 64].rearrange("(j b) c -> b j c", b=128)
                    dma("sp", vt2[slot][:, r * nb:(r + 1) * nb, h_ * 128:h_ * 128 + 64], src, [],
                        [("vtd", slot, r, h_)] + ([("vtdany", slot)] if (r == 0 and h_ == 0) else []))
            rope(qrs[slot], qa, qb, 128, (rt1, rt2), [["qa"], qbk], ("qr", slot))
            rope(krs[slot], ka, kb, 128, (rt1, rt2), [["ka"], kbk], ("kr", slot))
            for n in range(4):
                cols = slice(n * 512, (n + 1) * 512)
                for h_ in range(2):
                    act(kz[slot][h_][:, cols], krs[slot][:, cols], AF.Identity, [(("kr", slot), n), "cf"],
                        [("kz", slot, h_, n)], scale=half_m[h_])
        blocks = []
        units = [(sc, g) for sc in range(2) for g in range(3)]
        for ui, (sc, g) in enumerate(units):
            dl = DILS[g]
            L = SEQ // dl
            nb = L // 128
            for hf in range(2):
                for ti, (r, j) in enumerate([(r, j) for r in range(dl) for j in range(nb)]):
                    blocks.append(dict(sc=sc, g=g, hf=hf, r=r, j=j, dl=dl, L=L, nb=nb, ui=ui,
                                       newunit=(hf == 0 and ti == 0), lastunit=(g == 2 and hf == 1 and ti == 15)))
        dil_prep(0, 0, 0)

        def dl_s1(b, i):
            if b["newunit"]:
                S.run_deferred(tasks)
                if b["ui"] + 1 < len(units):
                    nsc, ng = units[b["ui"] + 1]
                    record(dil_prep, nsc, ng, (b["ui"] + 1) % 2)
            slot = b["ui"] % 2
            dl, j, r, nb = b["dl"], b["j"], b["r"], b["nb"]
            N = 256 if j < nb - 1 else 128
            st0 = r + dl * 128 * j
            kcols = slice(st0, st0 + dl * 127 + 1, dl)
            qcols = slice(st0, st0 + dl * (N - 1) + 1, dl)
            if FILLER:
                mm(ps[6][:, 0:512], ident, ropeC[:, 0:512], True, True, ["cb"], [PK(6)], False, skip=True)
            mm(ps[i % 2][:, 0:N], ident, band2[:, 0:N], True, False, ["cb"], [PK(i % 2)], False, skip=True)
            mm(ps[i % 2][:, 0:N], kz[slot][b["hf"]][:, kcols], qrs[slot][:, qcols], False, True,
               [("kz", slot, b["hf"], n) for n in range(4)] + [(("qr", slot), n) for n in range(4)], [PK(i % 2)],
               True, skip=True)

        def dl_s2(b, i):
            N = 256 if b["j"] < b["nb"] - 1 else 128
            act(Pb[i % 4][:, 0:N], ps[i % 2][:, 0:N], AF.Exp, [PK(i % 2)], [("Pb", i % 4)])

        def dl_s3(b, i):
            slot = b["ui"] % 2
            hf, g, sc = b["hf"], b["g"], b["sc"]
            dl, j, r, nb, L = b["dl"], b["j"], b["r"], b["nb"], b["L"]
            nblk = 2 if j < nb - 1 else 1
            u0 = r * L + 128 * j
            vi = r * nb + j
            for blk in range(nblk):
                u = u0 + 128 * blk
                q4 = u // 512
                col = u % 512
                bk = 2 + (q4 % 2)
                opening = (blk == 1) or (j == 0)
                closing = (blk == 0)
                mm(ps[bk][:, col:col + 128], vt2[slot][:, vi, hf * 64:hf * 64 + 128],
                   Pb[i % 4][:, blk * 128:(blk + 1) * 128], opening, closing,
                   [("vtd", slot, r, 0), ("vtd", slot, r, 1), ("vtdany", slot), ("vtd1", slot), ("Pb", i % 4)],
                   [PK(bk)], True, skip=True)
                if closing and col == 384:
                    accx = ACC[hf]
                    if g == 0:
                        cp("dve", accx[:, q4 * 512:(q4 + 1) * 512], ps[bk][:, :], [PK(bk)], [("ACC", hf)])
                    else:
                        if g == 1:
                            dst = accx[:, q4:SEQ:4]
                            src_ = ps[bk][:, :]
                        else:
                            dst = accx[:, :].rearrange("p (m r) -> p r m", r=16)[:, 4 * q4:4 * q4 + 4, :]
                            src_ = ps[bk][:, :].rearrange("p (r m) -> p r m", r=4)
                        tt("dve", dst, src_, dst, ALU.add, [PK(bk), ("ACC", hf)], [("ACC", hf)])
            if b["lastunit"] and j == nb - 1 and r == dl - 1:
                lo, hi = slice(0, 64), slice(64, 128)
                for n in range(4):
                    cols = slice(n * 512, (n + 1) * 512)
                    R = Rt[n % 2]
                    act(R[lo, :], ACC[1][lo, cols], AF.Ln, [("ACC", 1)], [("Rt", n % 2)])
                    act(R[hi, :], ACC[0][hi, cols], AF.Ln, [("ACC", 0)], [("Rt", n % 2)])
                    act(R[:, :], R[:, :], AF.Exp, [("Rt", n % 2)], [("Rt", n % 2)], scale=-1.0)
                    mm(ps[4 + n % 2][:, :], swap_f, R[:, :], True, True, [("Rt", n % 2), "cf"], [PK(4 + n % 2)], True)
                    cp("act", Rs[n % 2][:, :], ps[4 + n % 2][:, :], [PK(4 + n % 2)], [("Rs", n % 2)])
                    tt("dve", oT[lo, sc, cols], ACC[0][lo, cols], Rs[n % 2][lo, :], ALU.mult,
                       [("ACC", 0), ("Rs", n % 2)], [("oT", sc, n, 0)])
                    tt("pool", oT[hi, sc, cols], ACC[1][hi, cols], Rs[n % 2][hi, :], ALU.mult,
                       [("ACC", 1), ("Rs", n % 2)], [("oT", sc, n, 1)])
        pipeline(blocks, [(dl_s1, 0), (dl_s2, 1), (dl_s3, 2)], drip=2)
        S.barrier()
        if stop_after == "dil":
            dump_oT()
            break

        if dbg and l == 0:
            S.barrier()
            dump_oT()
        S.barrier()
        if stop_after == "attn":
            break

        A = arena()
        oT = A("oT", [128, 8, SEQ], BF16, 8 * SEQ * 2)
        wbr = A("wbr", [128, 8, D], BF16, 8 * D * 2)
        wout = A("wout", [128, 8, D], BF16, 8 * D * 2)
        gst = [A(f"gst{i}", [128, 3, 512], BF16, 3 * 512 * 2) for i in range(2)]
        mgs = [A(f"mg{i}", [128, 8, 512], BF16, 8 * 512 * 2) for i in range(2)]
        et = [A(f"et{i}", [128, 512], BF16, 1024) for i in range(9)]
        dma("pool", wbr[:, 0:2, :], w_br_a[l].rearrange("(kc p) c -> p kc c", p=128), [], [("wbr", 0)])
        dma("pool", wbr[:, 2:5, :], w_br_b[l].rearrange("(kc p) c -> p kc c", p=128), [], [("wbr", 1)])
        dma("pool", wbr[:, 5:8, :], w_br_c[l].rearrange("(kc p) c -> p kc c", p=128), [], [("wbr", 2)])
        dma("pool", wout[:, 0:4, :], w_out[l, 0:512, :].rearrange("(kc p) c -> p kc c", p=128), [], [("wout", 0)])
        dma("pool", wout[:, 4:8, :], w_out[l, 512:1024, :].rearrange("(kc p) c -> p kc c", p=128), [], [("wout", 1)])
        g_v = g_s.rearrange("(i j p) t -> j p i t", i=3, p=128)
        br_rng = ((0, 2), (2, 5), (5, 8))
        ei = [0]
        def outproj(n):
            cols = slice(n * 512, (n + 1) * 512)
            mg = mgs[n % 2]
            for j2 in range(8):
                b = 4 + (j2 % 2)
                for j in range(8):
                    mm(ps[b][:, :], wout[:, j, j2 * 128:(j2 + 1) * 128], mg[:, j, :], j == 0, j == 7,
                       [("wout", j // 4), ("mg", n % 2, j)], [PK(b)], j == 7)
                stt(xT[:, j2, cols], ps[b][:, :], modT[:, l, 16 + j2:17 + j2], xT[:, j2, cols], ALU.mult, ALU.add,
                    [PK(b), ("modT", l), XK(j2, n)], [XK(j2, n)])
        pend_add = [None]
        for n in range(4):
            cols = slice(n * 512, (n + 1) * 512)
            mg = mgs[n % 2]
            for j in range(8):
                gt = gst[j % 2]
                dma("sp", gt[:, :, :], g_v[j][:, :, cols], [("g_s", i * 8 + j) for i in range(3)], [("gst", j % 2)])
                bset = (0, 1, 2) if j % 2 == 0 else (3, 6, 7)
                for i, (k0, k1) in enumerate(br_rng):
                    for kc in range(k0, k1):
                        mm(ps[bset[i]][:, :], wbr[:, kc, j * 128:(j + 1) * 128], oT[:, kc, cols], kc == k0, kc == k1 - 1,
                           [("wbr", i), ("oT", kc, n)], [PK(bset[i])], kc == k1 - 1)
                i0 = 3 * ((n * 8 + j) % 3)
                e0, e1, e2 = et[i0], et[i0 + 1], et[i0 + 2]
                tt("dve", e0[:], ps[bset[0]][:, :], gt[:, 0, :], ALU.mult, [PK(bset[0]), ("gst", j % 2)], [("et", i0)])
                tt("dve", e1[:], ps[bset[1]][:, :], gt[:, 1, :], ALU.mult, [PK(bset[1]), ("gst", j % 2)], [("et", i0 + 1)])
                tt("dve", e2[:], ps[bset[2]][:, :], gt[:, 2, :], ALU.mult, [PK(bset[2]), ("gst", j % 2)], [("et", i0 + 2)])
                if pend_add[0] is not None:
                    pend_add[0]()
                pend_add[0] = (lambda e0=e0, e1=e1, e2=e2, i0=i0, mg=mg, j=j, n=n: (
                    tt("dve", e0[:], e0[:], e1[:], ALU.add, [("et", i0), ("et", i0 + 1)], [("et", i0)]),
                    tt("dve", mg[:, j, :], e0[:], e2[:], ALU.add, [("et", i0), ("et", i0 + 2)], [("mg", n % 2, j)])))
                if n > 0 and j == 4:
                    outproj(n - 1)
        pend_add[0]()
        outproj(3)
        if dbg and l == 0:
            for kc in range(8):
                dma("sp", x1_dbg[kc * 128:(kc + 1) * 128, :], xT[:, kc, :], [XK(kc, n) for n in range(4)], ["x1_dbg"])
        S.barrier()
        if stop_after == "mix":
            break

        A = arena()
        hT = A("hT", [128, 8, SEQ], BF16, 8 * SEQ * 2)
        actT = A("actT", [128, NFC, 1024], BF16, NFC * 1024 * 2)
        wgu = [A(f"wgu{i}", [128, 8, 512], BF16, 8 * 512 * 2) for i in range(2)]
        wdn = [A(f"wdn{i}", [128, NFC, 128], BF16, NFC * 128 * 2) for i in range(2)]
        sil = [A(f"sil{i}", [128, 512], F32, 2048) for i in range(2)]
        browf = A("browf", [48, 128], F32, 512)
        bTf = A("bTf", [128, 48], F32, 192)
        norm_off = A.o[0]
        norm_phase(hT, AB[:, l, 8:16], modT[:, l, 24:32], [("AB", l, 1), ("modT", l)], A, "b")
        atasks = []
        if l + 1 < n_layers:
            acc_hf = alloc_at("acchf", [128, 3 * D], F32, norm_off)
            wapf = [alloc_at(f"wapf{i}", [128, 512], F32, norm_off + 3 * D * 4 + i * 2048) for i in range(2)]
            assert norm_off + 3 * D * 4 + 4096 <= A.o[0]
            S.defer = atasks
            emit_adaln(l + 1, acc_hf, wapf, browf, bTf, [("hT", kc, 3) for kc in range(8)], 6, 7)
            S.defer = None
        w_gu_v = w_gu[l].rearrange("(kc p) c -> p kc c", p=128)
        w_dn_v = w_down[l].rearrange("(f p) c -> p f c", p=128)
        fi = [0]
        for half in range(2):
            def load_gu(fg):
                dma("pool", wgu[fg % 2][:, :, 0:256], w_gu_v[:, :, fg * 256:(fg + 1) * 256], [], [("wgu", fg % 2, 0)])
                dma("pool", wgu[fg % 2][:, :, 256:512], w_gu_v[:, :, DFF + fg * 256:DFF + (fg + 1) * 256], [], [("wgu", fg % 2, 1)])
            load_gu(0)
            for fg in range(11):
                if fg + 1 < 11:
                    load_gu(fg + 1)
                wt = wgu[fg % 2]
                for cc in range(2):
                    f = fg * 2 + cc
                    for nn in range(2):
                        n = half * 2 + nn
                        cols = slice(n * 512, (n + 1) * 512)
                        bg = (fi[0] % 2) * 2
                        bu = bg + 1
                        fi[0] += 1
                        for kc in range(8):
                            mm(ps[bg][:, :], wt[:, kc, cc * 128:(cc + 1) * 128], hT[:, kc, cols], kc == 0, kc == 7,
                               [("wgu", fg % 2, 0), ("hT", kc, n)], [PK(bg)], kc == 7)
                        for kc in range(8):
                            mm(ps[bu][:, :], wt[:, kc, 256 + cc * 128:256 + (cc + 1) * 128], hT[:, kc, cols], kc == 0, kc == 7,
                               [("wgu", fg % 2, 1), ("hT", kc, n)], [PK(bu)], kc == 7)
                        S.run_deferred(atasks, 2)
                        sl = sil[fi[0] % 2]
                        act(sl[:], ps[bg][:, :], AF.Silu, [PK(bg)], [("sil", fi[0] % 2)])
                        tt("dve", actT[:, f, nn * 512:(nn + 1) * 512], ps[bu][:, :], sl[:], ALU.mult,
                           [PK(bu), ("sil", fi[0] % 2)], [("actT", f, nn)])
            dma("pool", wdn[0][:], w_dn_v[:, :, 0:128], [], [("wdn", 0)])
            for j in range(8):
                if j + 1 < 8:
                    dma("pool", wdn[(j + 1) % 2][:], w_dn_v[:, :, (j + 1) * 128:(j + 2) * 128], [], [("wdn", (j + 1) % 2)])
                for nn in range(2):
                    n = half * 2 + nn
                    cols = slice(n * 512, (n + 1) * 512)
                    b = 4 + ((j * 2 + nn) % 2)
                    S.run_deferred(atasks, 3)
                    for f in range(NFC):
                        mm(ps[b][:, :], wdn[j % 2][:, f, :], actT[:, f, nn * 512:(nn + 1) * 512], f == 0, f == NFC - 1,
                           [("wdn", j % 2), ("actT", f, nn)], [PK(b)], f == NFC - 1)
                    stt(xT[:, j, cols], ps[b][:, :], modT[:, l, 40 + j:41 + j], xT[:, j, cols], ALU.mult, ALU.add,
                        [PK(b), ("modT", l), XK(j, n)], [XK(j, n)])
        S.run_deferred(atasks)
        S.barrier()

    if stop_after is None:
        A = arena()
        norm_phase(None, fgT, None, ["fgT"], A, "f", f32_out=True)
    else:
        for kc in range(8):
            dma("sp", outT_d[kc * 128:(kc + 1) * 128, :], xT[:, kc, :], [XK(kc, n) for n in range(4)], [("out", kc)])
    S.barrier()
    S.emit()
    return nc, S


_CACHE = {}


def _prep_inputs(inputs):
    cbh, cfh = _host_consts()
    f = lambda a: np.ascontiguousarray(np.asarray(a, dtype=np.float32))
    shared = {
        "w_ada": f(inputs["w_ada"]), "b_ada": f(inputs["b_ada"]), "norm1_g": f(inputs["norm1_g"]),
        "w_in": f(inputs["w_in"]), "w_br_a": f(inputs["w_br_a"]), "w_br_b": f(inputs["w_br_b"]),
        "w_br_c": f(inputs["w_br_c"]), "w_gate": f(inputs["w_gate"]), "b_gate": f(inputs["b_gate"]),
        "w_out": f(inputs["w_out"]), "norm2_g": f(inputs["norm2_g"]), "w_gu": f(inputs["w_gu"]),
        "w_down": f(inputs["w_down"]), "final_g": f(inputs["final_g"]).reshape(1, D),
        "cb": cbh, "cf": cfh,
    }
    x = f(inputs["x"])
    c = f(inputs["c"])
    in_maps = []
    for b in range(8):
        m = dict(shared)
        m["xT"] = np.ascontiguousarray(x[b].T)
        m["c"] = np.ascontiguousarray(c[b].reshape(8, 128).T)
        in_maps.append(m)
    return in_maps


def kernel(**inputs):
    if "nc" not in _CACHE:
        _CACHE["nc"] = build()[0]
    nc = _CACHE["nc"]
    in_maps = _prep_inputs(inputs)
    res = run_bass_kernel_spmd(nc, in_maps, core_ids=list(range(8)))
    out = np.stack([np.ascontiguousarray(res.results[b]["outT"].T) for b in range(8)], axis=0)
    return out.astype(np.float32)
```
